# Optimizing a Trainium2 kernel written in Bass

```python
import math
import jax, jax.numpy as jnp
from jax import lax
import numpy as np

D_MODEL = 1024
BATCH = 8
SEQ = 2048
DEPTH = 2

HG_HEADS = 4
HG_DK = 128
HG_DV = 128
HG_WIDTH = HG_HEADS * HG_DK
HG_CHUNK = 64
AT_HEADS = 8
AT_KV_HEADS = 2
AT_HEAD_DIM = 64
AT_WIDTH = AT_HEADS * AT_HEAD_DIM
AT_KV_WIDTH = AT_KV_HEADS * AT_HEAD_DIM
WINDOW = 128
ATT_BLOCK = 128
N_BRANCH = 2
D_FF = 2816
N_NORMS = 6
EPS = 1e-6
IN_SIZES = (HG_WIDTH, HG_WIDTH, HG_WIDTH, HG_WIDTH,
            AT_WIDTH, AT_KV_WIDTH, AT_KV_WIDTH, D_MODEL, D_MODEL)
D_IN = sum(IN_SIZES)
IN_SPLITS = tuple(int(v) for v in np.cumsum(IN_SIZES)[:-1])

kernel_name = "hybrid_hgrn2_swa_sink_macaron"


def rms_norm(x, gain):
    xf = x.astype(jnp.float32)
    y = xf * lax.rsqrt(jnp.mean(xf * xf, axis=-1, keepdims=True) + EPS)
    return (y * gain.astype(jnp.float32)).astype(x.dtype)


def swiglu(h, w_gate, w_up, w_down):
    return (jax.nn.silu(h @ w_gate) * (h @ w_up)) @ w_down


def alibi_slopes(n_heads):
    idx = jnp.arange(1, n_heads + 1, dtype=jnp.float32)
    return jnp.exp2(-8.0 * idx / n_heads)


def hgrn2_mixer(q, f_logit, i, g_out, lb, head_gain):
    B, S, _ = q.shape
    dt = q.dtype
    nc = S // HG_CHUNK
    lb = lb.astype(jnp.float32)
    z = f_logit.astype(jnp.float32)
    log_f = jnp.logaddexp(jnp.log(lb), jnp.log1p(-lb) + jax.nn.log_sigmoid(z))
    k = (1.0 - lb) * jax.nn.sigmoid(-z)
    qf = jax.nn.silu(q.astype(jnp.float32))
    vf = i.astype(jnp.float32)

    def to_chunks(t, d):
        return t.reshape(B, nc, HG_CHUNK, HG_HEADS, d).transpose(1, 0, 3, 2, 4)

    qc, kc, vc, gc = (to_chunks(qf, HG_DK), to_chunks(k, HG_DK),
                      to_chunks(vf, HG_DV), to_chunks(log_f, HG_DK))
    causal = jnp.tril(jnp.ones((HG_CHUNK, HG_CHUNK), dtype=bool))

    def step(state, inp):
        qb, kb, vb, gb = inp
        G = jnp.cumsum(gb, axis=2)
        o_inter = jnp.einsum('bhtd,bhde->bhte', qb * jnp.exp(G), state)
        diff = G[:, :, :, None, :] - G[:, :, None, :, :]
        decay = jnp.exp(jnp.where(causal[:, :, None], diff, -jnp.inf))
        scores = jnp.einsum('bhtd,bhtsd,bhsd->bhts', qb, decay, kb)
        o = o_inter + jnp.einsum('bhts,bhse->bhte', scores, vb)
        G_last = G[:, :, -1:, :]
        new_state = (jnp.exp(G_last[:, :, 0, :])[..., None] * state
                     + jnp.einsum('bhsd,bhse->bhde', kb * jnp.exp(G_last - G), vb))
        return new_state, o

    state0 = jnp.zeros((B, HG_HEADS, HG_DK, HG_DV), jnp.float32)
    _, o = lax.scan(step, state0, (qc, kc, vc, gc))
    o = o.transpose(1, 0, 3, 2, 4).reshape(B, S, HG_HEADS, HG_DV)
    o = o * lax.rsqrt(jnp.mean(o * o, axis=-1, keepdims=True) + EPS) * head_gain.astype(jnp.float32)
    o = o.reshape(B, S, HG_WIDTH) * jax.nn.silu(g_out.astype(jnp.float32))
    return o.astype(dt)


def swa_sink_attention(q, k, v, sinks):
    B, S, _ = q.shape
    dt = q.dtype
    L = ATT_BLOCK
    nb = S // L
    G = AT_HEADS // AT_KV_HEADS
    qb = q.reshape(B, nb, L, AT_KV_HEADS, G, AT_HEAD_DIM)
    kb = k.reshape(B, nb, L, AT_KV_HEADS, AT_HEAD_DIM)
    vb = v.reshape(B, nb, L, AT_KV_HEADS, AT_HEAD_DIM)
    pad = ((0, 0), (1, 0), (0, 0), (0, 0), (0, 0))
    kk = jnp.concatenate([jnp.pad(kb[:, :-1], pad), kb], axis=2)
    vv = jnp.concatenate([jnp.pad(vb[:, :-1], pad), vb], axis=2)
    s = jnp.einsum('bnqhgd,bnkhd->bnhgqk', qb, kk).astype(jnp.float32) * (AT_HEAD_DIM ** -0.5)
    qi = jnp.arange(L)[:, None] + L
    kj = jnp.arange(2 * L)[None, :]
    dist = qi - kj
    in_window = (dist >= 0) & (dist < WINDOW)
    key_exists = (jnp.arange(nb)[:, None] * L + kj - L) >= 0
    mask = in_window[None] & key_exists[:, None, :]
    slopes = alibi_slopes(AT_HEADS).reshape(AT_KV_HEADS, G)
    s = s - slopes[:, :, None, None] * dist.astype(jnp.float32)
    s = jnp.where(mask[None, :, None, None], s, -jnp.inf)
    sink = jnp.broadcast_to(sinks.astype(jnp.float32).reshape(AT_KV_HEADS, G)[None, None, :, :, None, None],
                            s.shape[:-1] + (1,))
    p = jax.nn.softmax(jnp.concatenate([s, sink], axis=-1), axis=-1)[..., :-1]
    out = jnp.einsum('bnhgqk,bnkhd->bnqhgd', p.astype(dt), vv)
    return out.reshape(B, S, AT_WIDTH)


def setup_inputs(seed: int = 0) -> dict:
    key = jax.random.key(seed)
    ks = jax.random.split(key, 16)
    f32 = jnp.float32

    def w(k, shape, fan_in):
        return jax.random.normal(k, shape, f32) * (fan_in ** -0.5)

    return {
        "x": jax.random.normal(ks[0], (BATCH, SEQ, D_MODEL), f32),
        "norm_gains": 1.0 + 0.1 * jax.random.normal(ks[1], (DEPTH, N_NORMS, D_MODEL), f32),
        "w_ffn1_gate": w(ks[2], (DEPTH, D_MODEL, D_FF), D_MODEL),
        "w_ffn1_up": w(ks[3], (DEPTH, D_MODEL, D_FF), D_MODEL),
        "w_ffn1_down": w(ks[4], (DEPTH, D_FF, D_MODEL), D_FF),
        "w_in": w(ks[5], (DEPTH, D_MODEL, D_IN), D_MODEL),
        "hgrn_lower_bounds": jax.random.normal(ks[6], (DEPTH, HG_WIDTH), f32),
        "hgrn_head_gain": 1.0 + 0.1 * jax.random.normal(ks[7], (DEPTH, HG_DV), f32),
        "attn_sinks": 0.5 * jax.random.normal(ks[8], (DEPTH, AT_HEADS), f32),
        "w_branch_hgrn": w(ks[9], (DEPTH, HG_WIDTH, D_MODEL), HG_WIDTH),
        "w_branch_attn": w(ks[10], (DEPTH, AT_WIDTH, D_MODEL), AT_WIDTH),
        "w_out": w(ks[11], (DEPTH, D_MODEL, D_MODEL), D_MODEL),
        "w_ffn2_gate": w(ks[12], (DEPTH, D_MODEL, D_FF), D_MODEL),
        "w_ffn2_up": w(ks[13], (DEPTH, D_MODEL, D_FF), D_MODEL),
        "w_ffn2_down": w(ks[14], (DEPTH, D_FF, D_MODEL), D_FF),
    }


def reference(x, norm_gains, w_ffn1_gate, w_ffn1_up, w_ffn1_down, w_in, hgrn_lower_bounds,
              hgrn_head_gain, attn_sinks, w_branch_hgrn, w_branch_attn, w_out,
              w_ffn2_gate, w_ffn2_up, w_ffn2_down):
    lb_all = jnp.cumsum(jax.nn.softmax(hgrn_lower_bounds.astype(jnp.float32), axis=0), axis=0)
    lb_all = lb_all - lb_all[0:1]
    for l in range(DEPTH):
        gn = norm_gains[l]
        h = rms_norm(x, gn[0])
        x = x + 0.5 * rms_norm(swiglu(h, w_ffn1_gate[l], w_ffn1_up[l], w_ffn1_down[l]), gn[1])
        h = rms_norm(x, gn[2])
        proj = h @ w_in[l]
        hq, hf, hi, hg, aq, ak, av, gate_a, gate_b = jnp.split(proj, IN_SPLITS, axis=-1)
        y_hg = hgrn2_mixer(hq, hf, hi, hg, lb_all[l], hgrn_head_gain[l]) @ w_branch_hgrn[l]
        y_at = swa_sink_attention(aq, ak, av, attn_sinks[l]) @ w_branch_attn[l]
        mixed = jax.nn.sigmoid(gate_a) * y_hg + jax.nn.sigmoid(gate_b) * y_at
        x = x + rms_norm(mixed @ w_out[l], gn[3])
        h = rms_norm(x, gn[4])
        x = x + 0.5 * rms_norm(swiglu(h, w_ffn2_gate[l], w_ffn2_up[l], w_ffn2_down[l]), gn[5])
    return x
```

```python
import numpy as np
from contextlib import ExitStack

import concourse.bass as bass
import concourse.mybir as mybir
from concourse.bass_utils import run_bass_kernel_spmd

F32 = mybir.dt.float32
BF16 = mybir.dt.bfloat16
AF = mybir.ActivationFunctionType
ALU = mybir.AluOpType

ENGS = ("pe", "act", "dve", "pool", "sp")

S = 2048
D = 1024
DFF = 2816
NFC = DFF // 128
DIN = 4864
DEPTH = 2
EPS = 1e-6
NT = S // 128
BT = 8
NB = NT // BT
BTOK = BT * 128


class Res:
    __slots__ = ("name", "lw", "rd", "rd_dma", "inherit")

    def __init__(self, name):
        self.name = name
        self.lw = None
        self.rd = {}
        self.rd_dma = []
        self.inherit = []


class Op:
    __slots__ = ("eng", "fn", "deps", "dma_key", "signal", "tick", "waits")

    def __init__(self, eng, fn, dma_key):
        self.eng = eng
        self.fn = fn
        self.deps = []
        self.dma_key = dma_key
        self.signal = False
        self.tick = 0
        self.waits = None


class Prog:
    def __init__(self):
        self.ops = []
        self.pending_bar = {e: [] for e in ENGS}
        self.last_on = {}
        self.dma_last = {}
        self.live = []

    def region(self, off, nbytes, names):
        s_, e_ = off, off + nbytes
        names = tuple(names)
        inherit = []
        keep = []
        for ent in self.live:
            a, b, nm, rs = ent
            if a < e_ and s_ < b:
                if a == s_ and b == e_ and nm == names:
                    return rs
                for r in rs:
                    if r.lw is not None:
                        inherit.append(r.lw)
                    inherit.extend(r.rd.values())
                    inherit.extend(r.rd_dma)
                    inherit.extend(r.inherit)
            else:
                keep.append(ent)
        inh = sorted(set(inherit))
        rs = []
        for n in names:
            r = Res(n)
            r.inherit = list(inh)
            rs.append(r)
        keep.append((s_, e_, names, rs))
        self.live = keep
        return rs

    def add(self, eng, fn, reads=(), writes=(), dma_key=None):
        i = len(self.ops)
        op = Op(eng, fn, dma_key)
        deps = {}

        def dep(j, kind):
            if j is None or j == i:
                return
            k = deps.get(j)
            if k is None or kind == "raw":
                deps[j] = kind

        for r in reads:
            dep(r.lw, "raw")
            for j in r.inherit:
                dep(j, "raw")
        for w in writes:
            for j in w.inherit:
                dep(j, "raw")
            w.inherit = []
            dep(w.lw, "waw")
            for j in w.rd.values():
                dep(j, "war")
            for j in w.rd_dma:
                dep(j, "war")
        for j in self.pending_bar[eng]:
            dep(j, "raw")
        self.pending_bar[eng] = []
        wset = set(id(w) for w in writes)
        for w in writes:
            w.lw = i
            w.rd = {}
            w.rd_dma = []
        for r in reads:
            if id(r) in wset:
                continue
            if dma_key is not None:
                r.rd_dma.append(i)
            else:
                r.rd[eng] = i
        op.deps = list(deps.items())
        self.ops.append(op)
        if dma_key is None:
            self.last_on[eng] = i
        else:
            self.dma_last[dma_key] = i
        return i

    def barrier(self):
        lst = list(self.last_on.values()) + list(self.dma_last.values())
        for e in ENGS:
            self.pending_bar[e] = list(lst)

    def finalize(self):
        ops = self.ops
        for i, op in enumerate(ops):
            keep = []
            for j, kind in op.deps:
                pj = ops[j]
                if pj.dma_key is not None:
                    keep.append(j)
                elif pj.eng == op.eng:
                    if op.dma_key is not None:
                        keep.append(j)
                    elif op.eng == "pe":
                        continue
                    elif kind == "raw":
                        keep.append(j)
                else:
                    keep.append(j)
            op.deps = keep
            for j in keep:
                ops[j].signal = True
        cnt = {}
        for op in ops:
            if op.dma_key is not None:
                k = ("dma", op.dma_key)
                cnt[k] = cnt.get(k, 0) + 16
                op.tick = cnt[k]
            elif op.signal:
                k = ("eng", op.eng)
                cnt[k] = cnt.get(k, 0) + 1
                op.tick = cnt[k]
        self.sem_keys = list(cnt.keys())
        seen = {e: {} for e in ENGS}
        for op in ops:
            need = {}
            s = seen[op.eng]
            for j in op.deps:
                pj = ops[j]
                k = ("dma", pj.dma_key) if pj.dma_key is not None else ("eng", pj.eng)
                if s.get(k, 0) >= pj.tick:
                    continue
                if need.get(k, 0) < pj.tick:
                    need[k] = pj.tick
            for k, v in need.items():
                s[k] = v
            op.waits = list(need.items())

    def emit(self, nc, final_waits=()):
        self.finalize()
        ops = self.ops
        with ExitStack() as st:
            sems = {}
            for n, k in enumerate(self.sem_keys):
                sems[k] = st.enter_context(nc.semaphore("sem%d" % n))
            block = st.enter_context(nc.Block())

            def run(engname):
                def body(e):
                    for op in ops:
                        if op.eng != engname:
                            continue
                        for k, v in op.waits:
                            e.wait_ge(sems[k], v)
                        ins = op.fn(e)
                        if op.dma_key is not None:
                            ins.then_inc(sems[("dma", op.dma_key)], 16)
                        elif op.signal:
                            ins.then_inc(sems[("eng", op.eng)], 1)
                    if engname == "sp":
                        for j in final_waits:
                            pj = ops[j]
                            e.wait_ge(sems[("dma", pj.dma_key)], pj.tick)
                return body

            block.tensor(run("pe"))
            block.scalar(run("act"))
            block.vector(run("dve"))
            block.gpsimd(run("pool"))
            block.sync(run("sp"))


ARENA_BYTES = 211968
import os as _os
HOIST_EARLY = _os.environ.get('K_HOIST_EARLY', '1') == '1'
HOIST_PRE = _os.environ.get('K_HOIST_PRE', '0') == '1'
MIXLVL = int(_os.environ.get('K_MIX', '4'))
HOIST_STAT = _os.environ.get('K_HOIST_STAT', '1') == '1'
RECLVL = int(_os.environ.get('K_REC', '9'))
ILV = _os.environ.get('K_ILV', '1') == '1'


def build_program(stop_after=None):
    nc = bass.Bass("TRN2", target_bir_lowering=False)

    def din(name, shape):
        return nc.dram_tensor(name, list(shape), F32, kind="ExternalInput").ap()

    x_d = din("x", [S, D])
    out_d = nc.dram_tensor("out", [S, D], F32, kind="ExternalOutput").ap()
    gcols_d = din("gcols", [128, 12, 8])
    gains_d = din("gains", [12, D])
    wg_d = [din("wg1", [DEPTH, D, DFF]), din("wg2", [DEPTH, D, DFF])]
    wu_d = [din("wu1", [DEPTH, D, DFF]), din("wu2", [DEPTH, D, DFF])]
    wd_d = [din("wd1", [DEPTH, DFF, D]), din("wd2", [DEPTH, DFF, D])]
    win_d = din("w_in", [DEPTH, D, DIN])
    wbh_d = din("w_bh", [DEPTH, 512, D])
    wba_d = din("w_ba", [DEPTH, 512, D])
    wout_d = din("w_out", [DEPTH, D, D])
    lbp_d = din("lbp", [128, 2, 4])
    hgain_d = din("hgain", [128, 2])
    sinks_d = din("sinks", [DEPTH, 8])
    ident_d = din("ident", [128, 128])
    maskc_d = din("maskc", [128, 128])
    notstart_d = din("notstart", [128, 512])
    abias_d = din("abias", [128, 2 * 8 * 128])

    P = Prog()
    st = ExitStack()
    arena = st.enter_context(nc.sbuf_tensor("arena", [128, ARENA_BYTES // 2], BF16))
    pairs = [st.enter_context(nc.psum_tensor("pp%d" % i, [128, 1024], F32)) for i in range(4)]

    def bank(i):
        return pairs[i // 2][:, (i % 2) * 512:(i % 2) * 512 + 512]

    def bankbf(i):
        return bank(i).bitcast(BF16)

    rbank = [Res("bank%d" % i) for i in range(8)]

    def view(off, nbytes, dt, pat=None, **kw):
        assert off % 4 == 0 and off + nbytes <= ARENA_BYTES, (off, nbytes)
        v = arena[:, off // 2:(off + nbytes) // 2]
        if dt == F32:
            v = v.bitcast(F32)
        if pat:
            v = v.rearrange(pat, **kw)
        return v

    def RG(off, nbytes, *names):
        return P.region(off, nbytes, names)

    def MM(out, lhsT, rhs, start, stop, reads, writes):
        P.add("pe", lambda e: e.matmul(out, lhsT=lhsT, rhs=rhs, start=start, stop=stop), reads, writes)

    def TR(out, in_, reads, writes):
        P.add("pe", lambda e: e.transpose(out, in_, ident), list(reads) + [r_ident], writes)

    def ACT(out, in_, func, reads, writes, **kw):
        P.add("act", lambda e: e.activation(out=out, in_=in_, func=func, **kw), reads, writes)

    def TT(out, in0, in1, op, reads, writes, eng="dve"):
        P.add(eng, lambda e: e.tensor_tensor(out=out, in0=in0, in1=in1, op=op), reads, writes)

    def TS(out, in0, s1, s2, op0, op1, reads, writes, eng="dve"):
        if s2 is None:
            P.add(eng, lambda e: e.tensor_scalar(out=out, in0=in0, scalar1=s1, scalar2=None, op0=op0), reads, writes)
        else:
            P.add(eng, lambda e: e.tensor_scalar(out=out, in0=in0, scalar1=s1, scalar2=s2, op0=op0, op1=op1), reads, writes)

    def STT(out, in0, scalar, in1, op0, op1, reads, writes):
        P.add("dve", lambda e: e.scalar_tensor_tensor(out=out, in0=in0, scalar=scalar, in1=in1, op0=op0, op1=op1), reads, writes)

    def CP(eng, out, in_, reads, writes):
        if eng == "act":
            P.add("act", lambda e: e.copy(out=out, in_=in_), reads, writes)
        else:
            P.add(eng, lambda e: e.tensor_copy(out=out, in_=in_), reads, writes)

    def MEMSET(eng, ap, val, writes):
        P.add(eng, lambda e: e.memset(ap, val), [], writes)

    def DMA(q, out, in_, key, reads, writes):
        return P.add(q, lambda e: e.dma_start(out=out, in_=in_), reads, writes, dma_key=key)

    OFF_X = 0
    OFF_GB = 65536
    OFF_MISC = OFF_GB + 8192
    OFF_NS = OFF_MISC + 3072
    OFF_ST = OFF_NS + 1024
    B0 = OFF_ST + 2048
    OFF_HN = B0 + 122880
    OFF_PT = B0 + 126976
    assert OFF_PT + 4096 <= ARENA_BYTES

    X = view(OFF_X, 65536, F32, "p (t d) -> p t d", t=NT)
    rX = [Res("X%d" % t) for t in range(NT)]
    GB = [view(OFF_GB + i * 4096, 4096, F32) for i in range(2)]
    rGB = [Res("GB%d" % i) for i in range(2)]
    ident = view(OFF_MISC + 0, 256, BF16)
    r_ident = Res("ident")
    ones_bf = view(OFF_MISC + 256, 256, BF16)
    r_ones = Res("ones")
    maskc = view(OFF_MISC + 512, 256, BF16)
    r_maskc = Res("maskc")
    gcols = view(OFF_MISC + 768, 384, F32, "p (g k) -> p g k", g=12)
    r_gcols = Res("gcols")
    stats = view(OFF_MISC + 1152, 256, F32)
    r_stats = [Res("stats%d" % i) for i in range(16)]
    lbp = view(OFF_MISC + 1408, 32, F32, "p (l h) -> p l h", l=2)
    omlc = view(OFF_MISC + 1440, 16, F32)
    nomlc = view(OFF_MISC + 1456, 16, F32)
    hgain = view(OFF_MISC + 1472, 8, F32)
    lbtmp = view(OFF_MISC + 1480, 16, F32)
    homl = view(OFF_MISC + 2256, 16, F32)
    nhoml = view(OFF_MISC + 2272, 16, F32)
    bif = view(OFF_MISC + 2288, 16, F32)
    hg05 = view(OFF_MISC + 2304, 4, F32)
    lnhoml = view(OFF_MISC + 2320, 16, F32)
    r_lb = Res("lb")
    expsink = view(OFF_MISC + 1536, 32, F32)
    r_sink = Res("sink")
    kTprev = view(OFF_MISC + 1600, 256, BF16)
    r_kTprev = Res("kTprev")
    Vprev = view(OFF_MISC + 1856, 264, BF16)[:, 0:130].rearrange("p (k d) -> p k d", k=2)
    r_Vprev = Res("Vprev")
    neghalf = view(OFF_MISC + 2248, 4, F32)
    r_neghalf = Res("neghalf")
    notstart = view(OFF_NS, 1024, BF16)
    r_ns = Res("notstart")
    state = view(OFF_ST, 2048, F32, "p (h e) -> p h e", h=4)
    r_state = [Res("state%d" % h) for h in range(4)]
    HN = [view(OFF_HN + i * 2048, 2048, BF16) for i in range(2)]
    r_hn = [Res("hn%d" % i) for i in range(2)]
    PT = view(OFF_PT, 4096, F32)
    r_pt = Res("pt")

    hT = view(B0, 16384, BF16, "p (k t) -> p k t", k=8)
    rhT = [Res("hT%d" % j) for j in range(BT)]

    for t in range(NT):
        DMA("sp", X[:, t, :], x_d[t * 128:(t + 1) * 128, :], ("x", t), [], [rX[t]])
    DMA("pool", ident, ident_d, "c_ident", [], [r_ident])
    DMA("pool", maskc, maskc_d, "c_maskc", [], [r_maskc])
    DMA("pool", notstart, notstart_d, "c_ns", [], [r_ns])
    DMA("sp", gcols, gcols_d, "c_gcols", [], [r_gcols])
    DMA("sp", lbp, lbp_d, "c_lbp", [], [r_lb])
    DMA("sp", hgain, hgain_d, "c_hgain", [], [r_lb])
    MEMSET("dve", ones_bf, 1.0, [r_ones])
    MEMSET("dve", neghalf, -0.5, [r_neghalf])

    tr_bank = [6]

    PT_junk = view(OFF_PT, 2048, BF16)

    def prenorm_stat(b, j, junk=None, rjunk=None):
        tt = b * BT + j
        sl = tt % 16
        sc = stats[:, sl * 4:sl * 4 + 4]
        rs = r_stats[sl]
        if junk is None:
            junk, rjunk = PT_junk, r_pt
        ACT(junk, X[:, tt, :], AF.Square, [rX[tt]], [rjunk, rs], accum_out=sc[:, 0:1])
        TS(sc[:, 1:2], sc[:, 0:1], 1.0 / D, EPS, ALU.mult, ALU.add, [rs], [rs])
        TT(sc[:, 2:3], sc[:, 1:2], neghalf, ALU.pow, [rs, r_neghalf], [rs], eng="pool")

    def prenorm_scale(b, j):
        tt = b * BT + j
        sl = tt % 16
        hn = HN[j % 2]
        rhn = r_hn[j % 2]
        sc = stats[:, sl * 4:sl * 4 + 4]
        rs = r_stats[sl]
        if j % 2 == 0 or (HOIST_STAT and j != 3):
            P.add("act", lambda e, o=hn, i_=X[:, tt, :], m=sc[:, 2:3]: e.mul(out=o, in_=i_, mul=m), [rX[tt], rs], [rhn])
        else:
            TS(hn, X[:, tt, :], sc[:, 2:3], None, ALU.mult, None, [rX[tt], rs], [rhn])

    def prenorm_elem(b, j):
        prenorm_stat(b, j)
        prenorm_scale(b, j)

    def prenorm_block(b, gi, stats_done=False):
        if not stats_done:
            prenorm_stat(b, 0)
            prenorm_stat(b, 1)
        for j in range(BT):
            prenorm_scale(b, j)
            if j + 2 < BT and not stats_done:
                prenorm_stat(b, j + 2)
            prenorm_tr(b, j, gi)

    def prenorm_tr(b, j, gi):
        hn = HN[j % 2]
        rhn = r_hn[j % 2]
        bi = tr_bank[0]
        tr_bank[0] = 6 if bi == 7 else 7
        pb = bankbf(bi).rearrange("p (k t) -> p k t", k=8)
        for k in range(8):
            TR(pb[:, k, :], hn[:, k * 128:(k + 1) * 128], [rhn], [rbank[bi]])
        TT(hT[:, :, j * 128:(j + 1) * 128], pb,
           gcols[:, gi, :].unsqueeze(2).to_broadcast([128, 8, 128]), ALU.mult,
           [rbank[bi], r_gcols], [rhT[j]])

    def postnorm(tt, pi, gbi, fac, eps_mult=1.0):
        sl = tt % 16
        sc = stats[:, sl * 4:sl * 4 + 4]
        rs = r_stats[sl]
        pr = pairs[pi][:, :]
        rpp = [rbank[2 * pi], rbank[2 * pi + 1]]
        ACT(PT, pr, AF.Square, rpp, [r_pt, rs], accum_out=sc[:, 0:1])
        f2 = 1.0 / (fac * fac)
        TS(sc[:, 1:2], sc[:, 0:1], f2 / D, EPS * eps_mult * f2, ALU.mult, ALU.add, [rs], [rs])
        TT(sc[:, 2:3], sc[:, 1:2], neghalf, ALU.pow, [rs, r_neghalf], [rs], eng="pool")
        STT(PT, pr, sc[:, 2:3], GB[gbi], ALU.mult, ALU.mult, rpp + [rs, rGB[gbi]], [r_pt])
        TT(X[:, tt, :], X[:, tt, :], PT, ALU.add, [rX[tt], r_pt], [rX[tt]])

    def load_gain(gi, gbi):
        DMA("sp", GB[gbi], gains_d[gi:gi + 1, :].partition_broadcast(128), ("gb", gbi), [], [rGB[gbi]])

    F_ACT = B0 + 16384
    F_WD = F_ACT + 45056
    F_WGU = F_WD + 45056
    F_SIL = F_WGU + 12288
    assert F_SIL + 4096 == OFF_HN
    actT = view(F_ACT, 45056, BF16, "p (c t) -> p c t", c=NFC)
    WD = view(F_WD, 45056, BF16, "p (c n) -> p c n", c=NFC)
    WGU = [view(F_WGU + s * 4096, 4096, BF16, "p (g k n) -> p g k n", g=2, k=8) for s in range(3)]
    SIL = [view(F_SIL + i * 2048, 2048, F32) for i in range(2)]

    class FFNUnit:
        def __init__(self, l, which, b):
            self.l, self.which, self.b = l, which, b
            self.gi_pre = l * 6 + (0 if which == 0 else 4)
            self.gi_post = self.gi_pre + 1
            self.wg = wg_d[which][l]
            self.wu = wu_d[which][l]
            self.wd = wd_d[which][l]
            self.early_done = False

        def res(self):
            self.r_act = [RG(F_ACT + c * 2048 + n * 1024, 1024, "act%d_%d" % (c, n))[0]
                          for c in range(NFC) for n in range(2)]
            self.r_wd = [RG(F_WD + i * 4096, 4096, "wd%d" % i)[0] for i in range(NFC // 2)]
            self.r_wgu = [RG(F_WGU + s * 4096, 4096, "wg%d" % s, "wu%d" % s) for s in range(3)]
            self.r_sil = [RG(F_SIL + i * 2048, 2048, "sil%d" % i)[0] for i in range(2)]

        def load_gu(self, c):
            s = c % 3
            DMA("pool", WGU[s][:, 0, :, :], self.wg[:, c * 128:(c + 1) * 128].rearrange("(k p) n -> p k n", p=128),
                ("wg", s), [], [self.r_wgu[s][0]])
            DMA("pool", WGU[s][:, 1, :, :], self.wu[:, c * 128:(c + 1) * 128].rearrange("(k p) n -> p k n", p=128),
                ("wu", s), [], [self.r_wgu[s][1]])

        def early(self):
            self.r_wgu = [RG(F_WGU + s * 4096, 4096, "wg%d" % s, "wu%d" % s) for s in range(3)]
            for c in range(3):
                self.load_gu(c)
            self.early_done = True

        def stageA(self):
            if self.b == 0:
                load_gain(self.gi_post, 0)
            if not self.early_done:
                self.early()
            self.res()
            r_act, r_wd, r_wgu, r_sil = self.r_act, self.r_wd, self.r_wgu, self.r_sil
            nwd = 0
            for c in range(NFC):
                s = c % 3
                for n in range(2):
                    st_ = (2 * c + n) % 3
                    bg, bu = 2 * st_, 2 * st_ + 1
                    for k in range(8):
                        MM(bank(bg), WGU[s][:, 0, k, :], hT[:, k, n * 512:(n + 1) * 512], k == 0, k == 7,
                           [r_wgu[s][0]] + rhT[4 * n:4 * n + 4], [rbank[bg]])
                    for k in range(8):
                        MM(bank(bu), WGU[s][:, 1, k, :], hT[:, k, n * 512:(n + 1) * 512], k == 0, k == 7,
                           [r_wgu[s][1]] + rhT[4 * n:4 * n + 4], [rbank[bu]])
                    ACT(SIL[n], bank(bg), AF.Silu, [rbank[bg]], [r_sil[n]])
                    TT(actT[:, c, n * 512:(n + 1) * 512], SIL[n], bank(bu), ALU.mult,
                       [r_sil[n], rbank[bu]], [r_act[c * 2 + n]])
                if nwd < NFC // 2:
                    i = nwd
                    DMA("pool", WD[:, 2 * i:2 * i + 2, :],
                        self.wd[2 * i * 128:(2 * i + 2) * 128, :].rearrange("(c p) n -> p c n", p=128),
                        ("wd", i), [], [r_wd[i]])
                    nwd += 1
                if c + 3 < NFC:
                    self.load_gu(c + 3)

        def stageB(self, j):
            tt = self.b * BT + j
            pi = j % 3
            n = j // 4
            for d2 in range(2):
                for c in range(NFC):
                    MM(pairs[pi][:, d2 * 512:(d2 + 1) * 512], actT[:, c, j * 128:(j + 1) * 128],
                       WD[:, c, d2 * 512:(d2 + 1) * 512], c == 0, c == NFC - 1,
                       [self.r_act[c * 2 + n], self.r_wd[c // 2]], [rbank[2 * pi + d2]])
            postnorm(tt, pi, 0, 0.5)

    M_WA = B0 + 16384
    M_WATT = M_WA + 32768
    M_OGT = M_WATT + 12288
    M_AOT = M_OGT + 8192
    SCR = M_AOT + 8192
    assert SCR + 45056 == OFF_HN
    Whg = view(M_WA, 32768, BF16, "p (h k w n) -> p h k w n", h=4, k=8, w=4)
    Wout = view(M_WA, 16384, BF16, "p (k n) -> p k n", k=8)
    M3S = []
    for s in range(2):
        o = M_WA + 16384 + s * 6144
        M3S.append(dict(
            off=o,
            ga=view(o, 2048, BF16, "p (k n) -> p k n", k=8),
            gb=view(o + 2048, 2048, BF16, "p (k n) -> p k n", k=8),
            bh=view(o + 4096, 1024, BF16, "p (k n) -> p k n", k=4),
            ba=view(o + 5120, 1024, BF16, "p (k n) -> p k n", k=4),
        ))
    Watt = view(M_WATT, 12288, BF16, "p (k n) -> p k n", k=8)
    ogT = view(M_OGT, 8192, BF16, "p (h t) -> p h t", h=4)
    aoT = view(M_AOT, 8192, BF16, "p (h t) -> p h t", h=4)

    def sigmoid_chain3(specs):
        for (T_, src, rsrc, rT_, neg) in specs:
            ACT(T_, src, AF.Exp, rsrc, [rT_], scale=(1.0 if neg else -1.0))
        for (T_, src, rsrc, rT_, neg) in specs:
            ACT(T_, T_, AF.Ln, [rT_], [rT_], bias=1.0)
        for (T_, src, rsrc, rT_, neg) in specs:
            ACT(T_, T_, AF.Exp, [rT_], [rT_], scale=-1.0)

    def mixer_layer_setup(l):
        if l == 0:
            MEMSET("dve", omlc, 1.0, [r_lb])
            MEMSET("dve", nomlc, -1.0, [r_lb])
        else:
            TT(lbtmp, lbp[:, 1, :], lbp[:, 0, :], ALU.subtract, [r_lb], [r_lb])
            ACT(lbtmp, lbtmp, AF.Exp, [r_lb], [r_lb])
            TS(lbtmp, lbtmp, 1.0, None, ALU.add, None, [r_lb], [r_lb])
            P.add("dve", lambda e: e.reciprocal(out=omlc, in_=lbtmp), [r_lb], [r_lb])
            TS(nomlc, omlc, -1.0, None, ALU.mult, None, [r_lb], [r_lb])
        TS(homl, omlc, 0.5, None, ALU.mult, None, [r_lb], [r_lb])
        TS(nhoml, omlc, -0.5, None, ALU.mult, None, [r_lb], [r_lb])
        TS(bif, omlc, -0.5, 1.0, ALU.mult, ALU.add, [r_lb], [r_lb])
        TS(hg05, hgain[:, l:l + 1], 0.5, None, ALU.mult, None, [r_lb], [r_lb])
        ACT(lnhoml, homl, AF.Ln, [r_lb], [r_lb])
        DMA("sp", expsink, sinks_d[l:l + 1, :].partition_broadcast(128), "c_sink", [], [r_sink])
        ACT(expsink, expsink, AF.Exp, [r_sink], [r_sink])
        MEMSET("dve", state.rearrange("p h e -> p (h e)"), 0.0, r_state)
        load_gain(l * 6 + 3, 1)

    class MixUnit:
        def __init__(self, l, b):
            self.l, self.b = l, b
            self.gi_pre = l * 6 + 2
            self.early_done = True

        def early(self):
            pass

        def m1(self):
            l, b = self.l, self.b
            win = win_d[l]
            r_whg = [RG(M_WA + h * 8192, 8192, "whg%d" % h)[0] for h in range(4)]
            for h in range(4):
                DMA("pool", Whg[:, h, :, :, :].rearrange("p k w n -> p k (w n)"),
                    win[:, h * 512:(h + 1) * 512].rearrange("(k p) n -> p k n", p=128), ("whg", h), [], [r_whg[h]])
            self.r_watt = RG(M_WATT, 12288, "watt")[0]
            DMA("pool", Watt, win[:, 2048:2816].rearrange("(k p) n -> p k n", p=128), "watt", [], [self.r_watt])
            self.r_ogT = RG(M_OGT, 8192, *["ogT%d" % j for j in range(BT)])
            r_ogT = self.r_ogT

            Tsets = [[view(SCR + i * 2048, 2048, F32) for i in range(6)]] * 2
            rTsets = [[RG(SCR + i * 2048, 2048, "T%d" % i)[0] for i in range(6)]] * 2
            Dm = view(SCR + 12288, 4096, F32, "p (e c) -> p e c", c=8)
            Zb = view(SCR + 16384, 4096, F32, "p (e c) -> p e c", c=8)
            Zs = view(SCR + 20480, 4096, F32, "p (e c) -> p e c", c=8)
            r_Dm = RG(SCR + 12288, 4096, "Dm")[0]
            r_Zb = RG(SCR + 16384, 4096, "Zb")[0]
            r_Zs = RG(SCR + 20480, 4096, "Zs")[0]
            MEMSET("dve", Dm[:, :, 0:1], 0.0, [r_Dm])
            UB = SCR + 24576

            def ubuf(set_, i):
                return UB + set_ * 5120 + i * 1024
            qtT = [view(ubuf(s_, 0), 1024, BF16) for s_ in range(2)]
            ktT = [view(ubuf(s_, 1), 1024, BF16) for s_ in range(2)]
            kt = [view(ubuf(s_, 2), 1024, BF16, "p (j d) -> p j d", j=4) for s_ in range(2)]
            sgT = [view(ubuf(s_, 3), 1024, BF16) for s_ in range(2)]
            vtm = [view(ubuf(s_, 4), 1024, BF16, "p (j e) -> p j e", j=4) for s_ in range(2)]
            r_u = [[RG(ubuf(s_, i), 1024, "u%d_%d" % (s_, i))[0] for i in range(5)] for s_ in range(2)]
            o2 = SCR + 34816
            scm = [view(o2 + i * 1024, 1024, BF16, "p (j t) -> p j t", j=4) for i in range(2)]
            r_scm = [RG(o2 + i * 1024, 1024, "scm%d" % i)[0] for i in range(2)]
            sqb = view(o2 + 2048, 1024, BF16)
            r_sqb = RG(o2 + 2048, 1024, "sqb")[0]
            Rn = view(o2 + 3072, 2048, F32)
            r_Rn = RG(o2 + 3072, 2048, "Rn")[0]
            otmp = view(o2 + 5120, 2048, F32)
            r_otmp = RG(o2 + 5120, 2048, "otmp")[0]
            decay = [view(o2 + 7168 + i * 32, 32, F32) for i in range(2)]
            r_decay = [RG(o2 + 7168 + i * 32, 32, "decay%d" % i)[0] for i in range(2)]
            SbfA = view(o2 + 7232, 2048, BF16, "p (c e) -> p c e", c=8)
            r_SbfA = RG(o2 + 7232, 2048, "SbfA")[0]
            assert o2 + 7232 + 2048 <= SCR + 45056

            def proj_v(u, h, g):
                tok0 = g * 512
                bv = bank(3).rearrange("p (j e) -> p j e", j=4)
                for jj in range(4):
                    for k in range(8):
                        MM(bv[:, jj, :], hT[:, k, tok0 + jj * 128: tok0 + (jj + 1) * 128], Whg[:, h, k, 2, :],
                           k == 0, k == 7, [rhT[4 * g + jj], r_whg[h]], [rbank[3]])
                    yield

            def proj_qfg(u, h, g):
                tok0 = g * 512
                rh = rhT[4 * g:4 * g + 4]
                n = 0
                for (bi, wi) in ((0, 0), (1, 1), (2, 3)):
                    for k in range(8):
                        MM(bank(bi), Whg[:, h, k, wi, :], hT[:, k, tok0:tok0 + 512], k == 0, k == 7,
                           rh + [r_whg[h]], [rbank[bi]])
                        n += 1
                        if n % 3 == 0:
                            yield

            def elem(u, h, g):
                s_ = u % 2
                T = Tsets[s_]
                rT = rTsets[s_]
                rq, rk, rkt, rsg, rv = r_u[s_]
                bv = bank(3).rearrange("p (j e) -> p j e", j=4)
                CP("act", vtm[s_], bv, [rbank[3]], [rv])
                ACT(T[0], bank(0), AF.Tanh, [rbank[0]], [rT[0]], scale=0.5)
                ACT(T[2], bank(1), AF.Tanh, [rbank[1]], [rT[2]], scale=-0.5)
                ACT(T[3], bank(2), AF.Tanh, [rbank[2]], [rT[3]], scale=0.5)
                ACT(T[4], T[2], AF.Ln, [rT[2], r_lb], [rT[4]], scale=nhoml[:, h:h + 1], bias=bif[:, h:h + 1])
                yield
                STT(T[1], T[0], 1.0, bank(0), ALU.add, ALU.mult, [rT[0], rbank[0]], [rT[1]])
                STT(sgT[s_], T[3], 1.0, bank(2), ALU.add, ALU.mult, [rT[3], rbank[2]], [rsg])
                yield "banks_free"
                P.add("dve", lambda e, o=T[5], d1=T[4]: e.tensor_tensor_scan(
                    out=o, data0=notstart, data1=d1, initial=0.0, op0=ALU.mult, op1=ALU.add),
                    [rT[4], r_ns], [rT[5]])
                yield
                ACT(T[4], T[5], AF.Exp, [rT[5]], [rT[4]])
                ACT(T[0], T[5], AF.Exp, [rT[5], r_lb], [rT[0]], scale=-1.0, bias=lnhoml[:, h:h + 1])
                ACT(decay[s_], T[5].rearrange("p (c t) -> p c t", t=64)[:, :, 63], AF.Exp, [rT[5]], [r_decay[s_]])
                yield
                STT(qtT[s_], T[1], 0.5, T[4], ALU.mult, ALU.mult, [rT[1], rT[4]], [rq])
                yield
                STT(ktT[s_], T[2], 1.0, T[0], ALU.add, ALU.mult, [rT[2], rT[0]], [rk])
                yield
                pb = bankbf(3).rearrange("p (j d) -> p j d", j=8)
                for jj in range(4):
                    TR(pb[:, jj, :], ktT[s_][:, jj * 128:(jj + 1) * 128], [rk], [rbank[3]])
                CP("act", kt[s_], pb[:, 0:4, :], [rbank[3]], [rkt])
                yield

            def rec(u, h, g):
                s_ = u % 2
                rq, rk, rkt, rsg, rv = r_u[s_]
                bs = bank(4).rearrange("p (j t) -> p j t", j=4)
                for jj in range(4):
                    tsl = slice(jj * 128, (jj + 1) * 128)
                    MM(bs[:, jj, :], ktT[s_][:, tsl], qtT[s_][:, tsl], True, True, [rk, rq], [rbank[4]])
                sc_ = scm[s_]
                TT(sc_, bs, maskc.unsqueeze(1).to_broadcast([128, 4, 128]), ALU.mult,
                   [rbank[4], r_maskc], [r_scm[s_]])
                bo = bank(5)
                ci = 0
                for ch in range(2):
                    ub = bank(6 + ch).rearrange("p (c e) -> p c e", c=4)
                    ps = slice(ch * 64, (ch + 1) * 64)
                    for jj in range(4):
                        MM(ub[:, jj, :], kt[s_][ps, jj, :], vtm[s_][ps, jj, :], True, True, [rkt, rv], [rbank[6 + ch]])
                yield
                if RECLVL < 2:
                    return
                dec = decay[s_]
                CP("act", SbfA[:, 0, :], state[:, h, :], [r_state[h]], [r_SbfA])
                CP("dve", Dm[:, :, 1:8], dec[:, 1:8].unsqueeze(1).to_broadcast([128, 128, 7]), [r_decay[s_]], [r_Dm])
                ub0 = bank(6).rearrange("p (c e) -> p c e", c=4)
                TT(ub0[:, 0, :], ub0[:, 0, :], state[:, h, :], ALU.add, [rbank[6], r_state[h]], [rbank[6]])
                Zb_v = Zb.rearrange("p e (jj ch) -> p ch jj e", ch=2)
                for ch in range(2):
                    ub = bank(6 + ch).rearrange("p (c e) -> p c e", c=4)
                    TT(Zb_v[:, ch], ub, dec[:, ch::2].unsqueeze(2).to_broadcast([128, 4, 128]), ALU.mult,
                       [rbank[6 + ch], r_decay[s_]], [r_Zb])
                yield
                P.add("dve", lambda e, o=Zs.rearrange("p e c -> p (e c)"), d0=Dm.rearrange("p e c -> p (e c)"),
                      d1=Zb.rearrange("p e c -> p (e c)"): e.tensor_tensor_scan(
                    out=o, data0=d0, data1=d1, initial=0.0, op0=ALU.mult, op1=ALU.add),
                    [r_Dm, r_Zb], [r_Zs])
                yield
                CP("act", SbfA[:, 1:8, :], Zs.rearrange("p e c -> p c e")[:, 0:7, :], [r_Zs], [r_SbfA])
                CP("dve", state[:, h, :], Zs[:, :, 7], [r_Zs], [r_state[h]])
                yield
                for c in range(8):
                    jj, ch = c // 2, c % 2
                    ps = slice(ch * 64, (ch + 1) * 64)
                    cols = slice(c * 64, (c + 1) * 64)
                    MM(bo[:, cols], vtm[s_][ps, jj, :], sc_[ps, jj, ps], True, False, [rv, r_scm[s_]], [rbank[5]])
                    MM(bo[:, cols], SbfA[:, c, :], qtT[s_][:, cols], False, True, [r_SbfA, rq], [rbank[5]])
                    if c % 2 == 1:
                        yield
                if RECLVL < 3:
                    return
                ACT(sqb, bo, AF.Square, [rbank[5]], [r_sqb])
                MM(bank(4), ones_bf, sqb, True, True, [r_ones, r_sqb], [rbank[4]])
                ACT(Rn, bank(4), AF.Ln, [rbank[4]], [r_Rn], scale=1.0 / 128, bias=EPS)
                ACT(Rn, Rn, AF.Exp, [r_Rn], [r_Rn], scale=-0.5)
                STT(otmp, bo, hg05[:, 0:1], Rn, ALU.mult, ALU.mult, [rbank[5], r_Rn, r_lb], [r_otmp])
                TT(ogT[:, h, g * 512:(g + 1) * 512], otmp, sgT[s_], ALU.mult, [r_otmp, rsg], r_ogT[4 * g:4 * g + 4])
                yield

            units = [(h, g) for h in range(4) for g in range(2)]
            NU = len(units)

            def drain(gen):
                for _ in gen:
                    pass

            def step(gen):
                try:
                    return next(gen), False
                except StopIteration:
                    return None, True

            drain(proj_v(0, *units[0]))
            drain(proj_qfg(0, *units[0]))
            drain(elem(0, *units[0]))
            if NU > 1:
                drain(proj_v(1, *units[1]))
                drain(proj_qfg(1, *units[1]))
            for i in range(NU):
                g_rec = rec(i, *units[i]) if MIXLVL >= 2 else iter(())
                g_el = elem(i + 1, *units[i + 1]) if i + 1 < NU else iter(())
                g_pq = proj_qfg(i + 2, *units[i + 2]) if i + 2 < NU else iter(())
                d_rec = d_el = d_pq = False
                pq_ok = False
                while not (d_rec and d_el and d_pq):
                    if not d_rec:
                        _, d_rec = step(g_rec)
                    if not d_el:
                        tag, d_el = step(g_el)
                        if tag == "banks_free" or d_el:
                            pq_ok = True
                    if i + 1 >= NU:
                        pq_ok = True
                    if pq_ok and not d_pq:
                        _, d_pq = step(g_pq)
                        if not d_pq:
                            _, d_pq = step(g_pq)
                if i + 2 < NU:
                    drain(proj_v(i + 2, *units[i + 2]))

        def m2(self):
            l, b = self.l, self.b
            r_watt = self.r_watt
            self.r_aoT = RG(M_AOT, 8192, *["aoT%d" % j for j in range(BT)])
            r_aoT = self.r_aoT
            qT = view(SCR + 0, 8192, BF16, "p (c t) -> p c t", c=4)
            kT = view(SCR + 8192, 2304, BF16)
            Vaug = view(SCR + 10752, 2560, BF16)[:, 0:1170].rearrange("p (j k d) -> p j k d", j=9, k=2)
            abst = view(SCR + 13312, 8192, F32)
            expb = view(SCR + 21504, 4096, BF16, "p (b h q) -> p b h q", b=2, h=8)
            etmp = [view(SCR + 25600 + i * 2048, 2048, F32) for i in range(2)]
            pT = [[[view(SCR + 29696 + ((s_ * 2 + kv) * 2 + kb) * 1024, 1024, BF16) for kb in range(2)]
                   for kv in range(2)] for s_ in range(2)]
            ao = [view(SCR + 37888 + i * 1024, 1024, BF16) for i in range(2)]
            den = view(SCR + 39936, 128, F32)
            r_qT = RG(SCR + 0, 8192, "qT0", "qT1")
            r_kT = RG(SCR + 8192, 2304, "kTp", "kT0", "kT1")
            r_V = RG(SCR + 10752, 2560, *["V%d" % j_ for j_ in range(9)])
            r_abst = RG(SCR + 13312, 8192, "abst")[0]
            r_expb = RG(SCR + 21504, 4096, "expb")[0]
            r_etmp = [RG(SCR + 25600 + i * 2048, 2048, "etmp%d" % i)[0] for i in range(2)]
            r_pT = [[[RG(SCR + 29696 + ((s_ * 2 + kv) * 2 + kb) * 1024, 1024, "pT%d%d%d" % (s_, kv, kb))[0]
                      for kb in range(2)] for kv in range(2)] for s_ in range(2)]
            r_ao = [RG(SCR + 37888 + i * 1024, 1024, "ao%d" % i)[0] for i in range(2)]
            r_den = RG(SCR + 39936, 128, "den")[0]

            DMA("sp", abst, abias_d, "abias", [], [r_abst])
            ACT(expb.rearrange("p b h q -> p (b h q)"), abst, AF.Exp, [r_abst], [r_expb])
            MEMSET("dve", Vaug[:, :, :, 64], 1.0, r_V)
            if b > 0:
                CP("dve", kT[:, 0:128], kTprev, [r_kTprev], [r_kT[0]])
                CP("dve", Vaug[:, 0, :, :], Vprev, [r_Vprev], [r_V[0]])
            pb_cell = [6]

            def proj_group(g):
                rh = rhT[4 * g:4 * g + 4]
                for cc in range(4):
                    pbi = pb_cell[0]
                    for k in range(8):
                        MM(bank(pbi), Watt[:, k, cc * 128:(cc + 1) * 128], hT[:, k, g * 512:(g + 1) * 512],
                           k == 0, k == 7, rh + [r_watt], [rbank[pbi]])
                    CP("act" if cc % 2 == 0 else "dve", qT[:, cc, g * 512:(g + 1) * 512], bank(pbi),
                       [rbank[pbi]], [r_qT[g]])
                    pb_cell[0] = 13 - pbi
                    yield
                pbi = pb_cell[0]
                for k in range(8):
                    MM(bank(pbi), Watt[:, k, 512:640], hT[:, k, g * 512:(g + 1) * 512],
                       k == 0, k == 7, rh + [r_watt], [rbank[pbi]])
                CP("act", kT[:, 128 + g * 512: 128 + (g + 1) * 512], bank(pbi), [rbank[pbi]], [r_kT[1 + g]])
                pb_cell[0] = 13 - pbi
                yield
                pbi = pb_cell[0]
                bv = bank(pbi).rearrange("p (j n) -> p j n", j=4)
                for jj in range(4):
                    for k in range(8):
                        MM(bv[:, jj, :], hT[:, k, g * 512 + jj * 128: g * 512 + (jj + 1) * 128], Watt[:, k, 640:768],
                           k == 0, k == 7, [rhT[4 * g + jj], r_watt], [rbank[pbi]])
                P.add("dve", lambda e, o=Vaug[:, 1 + 4 * g:5 + 4 * g, :, 0:64],
                      i_=bv.rearrange("p j (k d) -> p j k d", k=2): e.tensor_copy(out=o, in_=i_),
                      [rbank[pbi]], r_V[1 + 4 * g:5 + 4 * g])
                pb_cell[0] = 13 - pbi
                yield

            for _ in proj_group(0):
                pass
            gp1 = proj_group(1)
            gp1_done = [False]

            def step_p1(n):
                for _ in range(n):
                    if gp1_done[0]:
                        return
                    try:
                        next(gp1)
                    except StopIteration:
                        gp1_done[0] = True

            def att_stage1(j):
                tt = b * BT + j
                s_ = j % 2
                kbs = [1] if tt == 0 else [0, 1]
                for kv in range(2):
                    prt = slice(kv * 64, (kv + 1) * 64)
                    for kb in kbs:
                        bi = (kv * 2 + kb)
                        kc0 = (j + kb) * 128
                        MM(bank(bi), kT[prt, kc0:kc0 + 128], qT[prt, :, j * 128:(j + 1) * 128], True, True,
                           r_kT + [r_qT[j // 4]], [rbank[bi]])
                        et = etmp[(kv * 2 + kb) % 2]
                        ret = r_etmp[(kv * 2 + kb) % 2]
                        ACT(et, bank(bi), AF.Exp, [rbank[bi]], [ret], scale=0.125)
                        TT(pT[s_][kv][kb].rearrange("p (h q) -> p h q", h=4), et.rearrange("p (h q) -> p h q", h=4),
                           expb[:, kb, kv * 4:(kv + 1) * 4, :], ALU.mult, [ret, r_expb], [r_pT[s_][kv][kb]])

            def att_stage2(j):
                tt = b * BT + j
                s_ = j % 2
                kbs = [1] if tt == 0 else [0, 1]
                for kv in range(2):
                    bpv = bank(4 + kv)[:, 0:260].rearrange("p (h d) -> p h d", h=4)
                    for g4 in range(4):
                        for n_, kb in enumerate(kbs):
                            MM(bpv[:, g4, :], pT[s_][kv][kb][:, g4 * 128:(g4 + 1) * 128], Vaug[:, j + kb, kv, :],
                               n_ == 0, n_ == len(kbs) - 1, [r_pT[s_][kv][kb], r_V[j + kb]], [rbank[4 + kv]])
                    dn = den[:, kv * 8:kv * 8 + 4]
                    rd = den[:, kv * 8 + 4:kv * 8 + 8]
                    TT(dn, bpv[:, :, 64], expsink[:, kv * 4:(kv + 1) * 4], ALU.add, [rbank[4 + kv], r_sink], [r_den])
                    P.add("dve", lambda e, o=rd, i_=dn: e.reciprocal(out=o, in_=i_), [r_den], [r_den])
                    TT(ao[s_][:, kv * 256:(kv + 1) * 256].rearrange("p (h d) -> p h d", h=4), bpv[:, :, 0:64],
                       rd.unsqueeze(2).to_broadcast([128, 4, 64]), ALU.mult, [rbank[4 + kv], r_den], [r_ao[s_]])
                tb = 6 + (j % 2)
                pb = bankbf(tb).rearrange("p (c t) -> p c t", c=8)
                for cc in range(4):
                    TR(pb[:, cc, :], ao[s_][:, cc * 128:(cc + 1) * 128], [r_ao[s_]], [rbank[tb]])
                CP("act", aoT[:, :, j * 128:(j + 1) * 128], pb[:, 0:4, :], [rbank[tb]], [r_aoT[j]])


            att_stage1(0)
            for j in range(BT):
                if j + 1 < BT:
                    if j + 1 >= 4:
                        step_p1(100)
                    att_stage1(j + 1)
                att_stage2(j)
                if j < 3:
                    step_p1(2)
            step_p1(100)
            if b == 0:
                CP("dve", kTprev, kT[:, 1024:1152], [r_kT[2]], [r_kTprev])
                CP("dve", Vprev, Vaug[:, 8, :, :], [r_V[8]], [r_Vprev])

        def m3(self):
            l, b = self.l, self.b
            win = win_d[l]
            r_ogT, r_aoT = self.r_ogT, self.r_aoT
            self.mixT = view(SCR + 0, 16384, BF16, "p (c t) -> p c t", c=8)
            mixT = self.mixT
            self.r_mixT = [RG(SCR + c_ * 2048 + g_ * 1024, 1024, "mixT%d_%d" % (c_, g_))[0]
                           for c_ in range(8) for g_ in range(2)]
            r_mixT = self.r_mixT
            TM = [[view(SCR + 16384 + (s2 * 4 + i) * 2048, 2048, F32) for i in range(4)] for s2 in range(2)]
            r_TM = [[RG(SCR + 16384 + (s2 * 4 + i) * 2048, 2048, "TM%d_%d" % (s2, i))[0] for i in range(4)]
                    for s2 in range(2)]
            self.r_wout = [RG(M_WA + i * 8192, 8192, "wout%d" % i)[0] for i in range(2)]
            for i in range(2):
                DMA("pool", Wout[:, 4 * i:4 * i + 4, :],
                    wout_d[l][i * 512:(i + 1) * 512, :].rearrange("(k p) n -> p k n", p=128), ("wout", i), [], [self.r_wout[i]])
            r_m3s = [dict(zip(("ga", "gb", "bh", "ba"),
                              [RG(M3S[s]["off"], 2048, "ga%d" % s)[0], RG(M3S[s]["off"] + 2048, 2048, "gb%d" % s)[0],
                               RG(M3S[s]["off"] + 4096, 1024, "bh%d" % s)[0], RG(M3S[s]["off"] + 5120, 1024, "ba%d" % s)[0]]))
                     for s in range(2)]

            def load_m3(c):
                s = c % 2
                DMA("pool", M3S[s]["ga"], win[:, 2816 + c * 128: 2816 + (c + 1) * 128].rearrange("(k p) n -> p k n", p=128),
                    ("ga", s), [], [r_m3s[s]["ga"]])
                DMA("pool", M3S[s]["gb"], win[:, 3840 + c * 128: 3840 + (c + 1) * 128].rearrange("(k p) n -> p k n", p=128),
                    ("gb", s), [], [r_m3s[s]["gb"]])
                DMA("pool", M3S[s]["bh"], wbh_d[l][:, c * 128:(c + 1) * 128].rearrange("(k p) n -> p k n", p=128),
                    ("bh", s), [], [r_m3s[s]["bh"]])
                DMA("pool", M3S[s]["ba"], wba_d[l][:, c * 128:(c + 1) * 128].rearrange("(k p) n -> p k n", p=128),
                    ("ba", s), [], [r_m3s[s]["ba"]])

            load_m3(0)
            load_m3(1)
            it = 0
            for c in range(8):
                s = c % 2
                for g in range(2):
                    s2 = it % 2
                    it += 1
                    b4 = 4 * s2
                    rh = rhT[4 * g:4 * g + 4]
                    tsl = slice(g * 512, (g + 1) * 512)
                    for k in range(8):
                        MM(bank(b4), M3S[s]["ga"][:, k, :], hT[:, k, tsl], k == 0, k == 7, rh + [r_m3s[s]["ga"]], [rbank[b4]])
                    for k in range(8):
                        MM(bank(b4 + 1), M3S[s]["gb"][:, k, :], hT[:, k, tsl], k == 0, k == 7, rh + [r_m3s[s]["gb"]], [rbank[b4 + 1]])
                    for k in range(4):
                        MM(bank(b4 + 2), M3S[s]["bh"][:, k, :], ogT[:, k, tsl], k == 0, k == 3,
                           r_ogT[4 * g:4 * g + 4] + [r_m3s[s]["bh"]], [rbank[b4 + 2]])
                    for k in range(4):
                        MM(bank(b4 + 3), M3S[s]["ba"][:, k, :], aoT[:, k, tsl], k == 0, k == 3,
                           r_aoT[4 * g:4 * g + 4] + [r_m3s[s]["ba"]], [rbank[b4 + 3]])
                    Ta, Tb, U1, U2 = TM[s2]
                    rTa, rTb, rU1, rU2 = r_TM[s2]
                    ACT(Ta, bank(b4), AF.Tanh, [rbank[b4]], [rTa], scale=0.5)
                    ACT(Tb, bank(b4 + 1), AF.Tanh, [rbank[b4 + 1]], [rTb], scale=0.5)
                    STT(U1, Ta, 1.0, bank(b4 + 2), ALU.add, ALU.mult, [rTa, rbank[b4 + 2]], [rU1])
                    STT(U2, Tb, 1.0, bank(b4 + 3), ALU.add, ALU.mult, [rTb, rbank[b4 + 3]], [rU2])
                    TT(mixT[:, c, tsl], U1, U2, ALU.add, [rU1, rU2], [r_mixT[c * 2 + g]])
                if c + 2 < 8:
                    load_m3(c + 2)

        def stageA(self):
            if self.b == 0:
                mixer_layer_setup(self.l)
            self.m1()
            if MIXLVL >= 3:
                self.m2()
            if MIXLVL >= 4:
                self.m3()

        def stageB(self, j):
            if MIXLVL < 4:
                return
            tt = self.b * BT + j
            pi = j % 3
            g = j // 4
            for d2 in range(2):
                for c in range(8):
                    MM(pairs[pi][:, d2 * 512:(d2 + 1) * 512], self.mixT[:, c, j * 128:(j + 1) * 128],
                       Wout[:, c, d2 * 512:(d2 + 1) * 512], c == 0, c == 7,
                       [self.r_mixT[c * 2 + g], self.r_wout[c // 4]], [rbank[2 * pi + d2]])
            postnorm(tt, pi, 1, 1.0, eps_mult=4.0)

    units = []
    for l in range(DEPTH):
        names = ["L%d_0" % l, "L%d_1" % l, "L%d_2" % l]
        units += [FFNUnit(l, 0, 0), FFNUnit(l, 0, 1)]
        if stop_after == names[0]:
            break
        units += [MixUnit(l, 0), MixUnit(l, 1)]
        if stop_after == names[1]:
            break
        units += [FFNUnit(l, 1, 0), FFNUnit(l, 1, 1)]
        if stop_after == names[2]:
            break

    u0 = units[0]
    prenorm_block(u0.b, u0.gi_pre)
    for ui, u in enumerate(units):
        nxt = units[ui + 1] if ui + 1 < len(units) else None
        u.stageA()
        if nxt is not None and HOIST_PRE:
            prenorm_elem(nxt.b, 0)
        for j in range(BT):
            if nxt is not None and HOIST_PRE:
                if j + 1 < BT:
                    prenorm_elem(nxt.b, j + 1)
                prenorm_tr(nxt.b, j, nxt.gi_pre)
            u.stageB(j)
            if HOIST_STAT and nxt is not None:
                prenorm_stat(nxt.b, j, HN[0], r_hn[0])
            if HOIST_EARLY and nxt is not None and j == 3 and isinstance(nxt, FFNUnit):
                nxt.early()

        if nxt is not None and not HOIST_PRE:
            prenorm_block(nxt.b, nxt.gi_pre, stats_done=HOIST_STAT)

    fin = []
    for t in range(NT):
        fin.append(DMA("sp", out_d[t * 128:(t + 1) * 128, :], X[:, t, :], ("x", t), [rX[t]], []))
    P.emit(nc, final_waits=fin)
    st.close()
    return nc


def _consts():
    ident = np.eye(128, dtype=np.float32)
    s = np.arange(128)[:, None]
    t = np.arange(128)[None, :]
    maskc = ((s // 64 == t // 64) & (s <= t)).astype(np.float32)
    notstart = np.ones((128, 512), np.float32)
    notstart[:, ::64] = 0.0
    slopes = np.exp2(-8.0 * np.arange(1, 9, dtype=np.float32) / 8.0)
    k = np.arange(128)[:, None, None]
    q = np.arange(128)[None, None, :]
    ab = np.empty((128, 2, 8, 128), np.float32)
    for kb in range(2):
        dist = (q - k + (128 if kb == 0 else 0)).astype(np.float32)
        valid = (k > q) if kb == 0 else (k <= q)
        ab[:, kb] = np.where(valid, -slopes[None, :, None] * dist, -30000.0)
    return ident, maskc, notstart, ab.reshape(128, -1)


_PROG_CACHE = {}


def kernel(x, norm_gains, w_ffn1_gate, w_ffn1_up, w_ffn1_down, w_in, hgrn_lower_bounds,
           hgrn_head_gain, attn_sinks, w_branch_hgrn, w_branch_attn, w_out,
           w_ffn2_gate, w_ffn2_up, w_ffn2_down, _stop_after=None):
    f = lambda a: np.ascontiguousarray(np.asarray(a, dtype=np.float32))
    x = f(x)
    ng = f(norm_gains)
    gains = ng.reshape(12, D)
    gcols = np.ascontiguousarray(ng.reshape(12, 8, 128).transpose(2, 0, 1))
    w_in = f(w_in).copy()
    perm = []
    for c in range(4):
        perm += list(range(2048 + c * 64, 2048 + (c + 1) * 64))
        perm += list(range(2048 + (4 + c) * 64, 2048 + (5 + c) * 64))
    w_in[:, :, 2048:2560] = w_in[:, :, perm]
    permh = [w_ * 512 + h_ * 128 + n_ for h_ in range(4) for w_ in range(4) for n_ in range(128)]
    w_in[:, :, 0:2048] = w_in[:, :, permh]
    lbp = np.ascontiguousarray(f(hgrn_lower_bounds).reshape(2, 4, 128).transpose(2, 0, 1))
    hgain = np.ascontiguousarray(f(hgrn_head_gain).T)
    ident, maskc, notstart, abias = _consts()
    shared = {
        "gcols": gcols, "gains": gains,
        "wg1": f(w_ffn1_gate), "wu1": f(w_ffn1_up), "wd1": f(w_ffn1_down),
        "wg2": f(w_ffn2_gate), "wu2": f(w_ffn2_up), "wd2": f(w_ffn2_down),
        "w_in": w_in, "w_bh": f(w_branch_hgrn), "w_ba": f(w_branch_attn), "w_out": f(w_out),
        "lbp": lbp, "hgain": hgain, "sinks": f(attn_sinks),
        "ident": ident, "maskc": maskc, "notstart": notstart, "abias": abias,
    }
    key = _stop_after
    if key not in _PROG_CACHE:
        _PROG_CACHE[key] = build_program(_stop_after)
    nc = _PROG_CACHE[key]
    in_maps = [dict(shared, x=x[i]) for i in range(8)]
    res = run_bass_kernel_spmd(nc, in_maps, core_ids=list(range(8)))
    return np.stack([np.asarray(r["out"], dtype=np.float32) for r in res.results], axis=0)
```

```python
import numpy as np
from contextlib import ExitStack

import concourse.bass as bass
import concourse.mybir as mybir
from concourse.bass_utils import run_bass_kernel_spmd

F32 = mybir.dt.float32
BF16 = mybir.dt.bfloat16
AF = mybir.ActivationFunctionType
ALU = mybir.AluOpType

ENGS = ("pe", "act", "dve", "pool", "sp")

S = 2048
D = 1024
DFF = 2816
NFC = DFF // 128
DIN = 4864
DEPTH = 2
EPS = 1e-6
NT = S // 128
BT = 8
NB = NT // BT
BTOK = BT * 128


class Res:
    __slots__ = ("name", "lw", "rd", "rd_dma", "inherit")

    def __init__(self, name):
        self.name = name
        self.lw = None
        self.rd = {}
        self.rd_dma = []
        self.inherit = []


class Op:
    __slots__ = ("eng", "fn", "deps", "dma_key", "signal", "tick", "waits")

    def __init__(self, eng, fn, dma_key):
        self.eng = eng
        self.fn = fn
        self.deps = []
        self.dma_key = dma_key
        self.signal = False
        self.tick = 0
        self.waits = None


class Prog:
    def __init__(self):
        self.ops = []
        self.pending_bar = {e: [] for e in ENGS}
        self.last_on = {}
        self.dma_last = {}
        self.live = []

    def region(self, off, nbytes, names):
        s_, e_ = off, off + nbytes
        names = tuple(names)
        inherit = []
        keep = []
        for ent in self.live:
            a, b, nm, rs = ent
            if a < e_ and s_ < b:
                if a == s_ and b == e_ and nm == names:
                    return rs
                for r in rs:
                    if r.lw is not None:
                        inherit.append(r.lw)
                    inherit.extend(r.rd.values())
                    inherit.extend(r.rd_dma)
                    inherit.extend(r.inherit)
            else:
                keep.append(ent)
        inh = sorted(set(inherit))
        rs = []
        for n in names:
            r = Res(n)
            r.inherit = list(inh)
            rs.append(r)
        keep.append((s_, e_, names, rs))
        self.live = keep
        return rs

    def add(self, eng, fn, reads=(), writes=(), dma_key=None):
        i = len(self.ops)
        op = Op(eng, fn, dma_key)
        deps = {}

        def dep(j, kind):
            if j is None or j == i:
                return
            k = deps.get(j)
            if k is None or kind == "raw":
                deps[j] = kind

        for r in reads:
            dep(r.lw, "raw")
            for j in r.inherit:
                dep(j, "raw")
        for w in writes:
            for j in w.inherit:
                dep(j, "raw")
            w.inherit = []
            dep(w.lw, "waw")
            for j in w.rd.values():
                dep(j, "war")
            for j in w.rd_dma:
                dep(j, "war")
        for j in self.pending_bar[eng]:
            dep(j, "raw")
        self.pending_bar[eng] = []
        wset = set(id(w) for w in writes)
        for w in writes:
            w.lw = i
            w.rd = {}
            w.rd_dma = []
        for r in reads:
            if id(r) in wset:
                continue
            if dma_key is not None:
                r.rd_dma.append(i)
            else:
                r.rd[eng] = i
        op.deps = list(deps.items())
        self.ops.append(op)
        if dma_key is None:
            self.last_on[eng] = i
        else:
            self.dma_last[dma_key] = i
        return i

    def barrier(self):
        lst = list(self.last_on.values()) + list(self.dma_last.values())
        for e in ENGS:
            self.pending_bar[e] = list(lst)

    def finalize(self):
        ops = self.ops
        for i, op in enumerate(ops):
            keep = []
            for j, kind in op.deps:
                pj = ops[j]
                if pj.dma_key is not None:
                    keep.append(j)
                elif pj.eng == op.eng:
                    if op.dma_key is not None:
                        keep.append(j)
                    elif op.eng == "pe":
                        continue
                    elif kind == "raw":
                        keep.append(j)
                else:
                    keep.append(j)
            op.deps = keep
            for j in keep:
                ops[j].signal = True
        cnt = {}
        for op in ops:
            if op.dma_key is not None:
                k = ("dma", op.dma_key)
                cnt[k] = cnt.get(k, 0) + 16
                op.tick = cnt[k]
            elif op.signal:
                k = ("eng", op.eng)
                cnt[k] = cnt.get(k, 0) + 1
                op.tick = cnt[k]
        self.sem_keys = list(cnt.keys())
        seen = {e: {} for e in ENGS}
        for op in ops:
            need = {}
            s = seen[op.eng]
            for j in op.deps:
                pj = ops[j]
                k = ("dma", pj.dma_key) if pj.dma_key is not None else ("eng", pj.eng)
                if s.get(k, 0) >= pj.tick:
                    continue
                if need.get(k, 0) < pj.tick:
                    need[k] = pj.tick
            for k, v in need.items():
                s[k] = v
            op.waits = list(need.items())

    def emit(self, nc, final_waits=()):
        self.finalize()
        ops = self.ops
        with ExitStack() as st:
            sems = {}
            for n, k in enumerate(self.sem_keys):
                sems[k] = st.enter_context(nc.semaphore("sem%d" % n))
            block = st.enter_context(nc.Block())

            def run(engname):
                def body(e):
                    for op in ops:
                        if op.eng != engname:
                            continue
                        for k, v in op.waits:
                            e.wait_ge(sems[k], v)
                        ins = op.fn(e)
                        if op.dma_key is not None:
                            ins.then_inc(sems[("dma", op.dma_key)], 16)
                        elif op.signal:
                            ins.then_inc(sems[("eng", op.eng)], 1)
                    if engname == "sp":
                        for j in final_waits:
                            pj = ops[j]
                            e.wait_ge(sems[("dma", pj.dma_key)], pj.tick)
                return body

            block.tensor(run("pe"))
            block.scalar(run("act"))
            block.vector(run("dve"))
            block.gpsimd(run("pool"))
            block.sync(run("sp"))


ARENA_BYTES = 211968
import os as _os
HOIST_EARLY = _os.environ.get('K_HOIST_EARLY', '1') == '1'
HOIST_PRE = _os.environ.get('K_HOIST_PRE', '0') == '1'
MIXLVL = int(_os.environ.get('K_MIX', '4'))
HOIST_STAT = _os.environ.get('K_HOIST_STAT', '1') == '1'
RECLVL = int(_os.environ.get('K_REC', '9'))
ILV = _os.environ.get('K_ILV', '1') == '1'


def build_program(stop_after=None):
    nc = bass.Bass("TRN2", target_bir_lowering=False)

    def din(name, shape):
        return nc.dram_tensor(name, list(shape), F32, kind="ExternalInput").ap()

    x_d = din("x", [S, D])
    out_d = nc.dram_tensor("out", [S, D], F32, kind="ExternalOutput").ap()
    gcols_d = din("gcols", [128, 12, 8])
    gains_d = din("gains", [12, D])
    wg_d = [din("wg1", [DEPTH, D, DFF]), din("wg2", [DEPTH, D, DFF])]
    wu_d = [din("wu1", [DEPTH, D, DFF]), din("wu2", [DEPTH, D, DFF])]
    wd_d = [din("wd1", [DEPTH, DFF, D]), din("wd2", [DEPTH, DFF, D])]
    win_d = din("w_in", [DEPTH, D, DIN])
    wbh_d = din("w_bh", [DEPTH, 512, D])
    wba_d = din("w_ba", [DEPTH, 512, D])
    wout_d = din("w_out", [DEPTH, D, D])
    lbp_d = din("lbp", [128, 2, 4])
    hgain_d = din("hgain", [128, 2])
    sinks_d = din("sinks", [DEPTH, 8])
    ident_d = din("ident", [128, 128])
    maskc_d = din("maskc", [128, 128])
    notstart_d = din("notstart", [128, 512])
    abias_d = din("abias", [128, 2 * 8 * 128])

    P = Prog()
    st = ExitStack()
    arena = st.enter_context(nc.sbuf_tensor("arena", [128, ARENA_BYTES // 2], BF16))
    pairs = [st.enter_context(nc.psum_tensor("pp%d" % i, [128, 1024], F32)) for i in range(4)]

    def bank(i):
        return pairs[i // 2][:, (i % 2) * 512:(i % 2) * 512 + 512]

    def bankbf(i):
        return bank(i).bitcast(BF16)

    rbank = [Res("bank%d" % i) for i in range(8)]

    def view(off, nbytes, dt, pat=None, **kw):
        assert off % 4 == 0 and off + nbytes <= ARENA_BYTES, (off, nbytes)
        v = arena[:, off // 2:(off + nbytes) // 2]
        if dt == F32:
            v = v.bitcast(F32)
        if pat:
            v = v.rearrange(pat, **kw)
        return v

    def RG(off, nbytes, *names):
        return P.region(off, nbytes, names)

    def MM(out, lhsT, rhs, start, stop, reads, writes):
        P.add("pe", lambda e: e.matmul(out, lhsT=lhsT, rhs=rhs, start=start, stop=stop), reads, writes)

    def TR(out, in_, reads, writes):
        P.add("pe", lambda e: e.transpose(out, in_, ident), list(reads) + [r_ident], writes)

    def ACT(out, in_, func, reads, writes, **kw):
        P.add("act", lambda e: e.activation(out=out, in_=in_, func=func, **kw), reads, writes)

    def TT(out, in0, in1, op, reads, writes, eng="dve"):
        P.add(eng, lambda e: e.tensor_tensor(out=out, in0=in0, in1=in1, op=op), reads, writes)

    def TS(out, in0, s1, s2, op0, op1, reads, writes, eng="dve"):
        if s2 is None:
            P.add(eng, lambda e: e.tensor_scalar(out=out, in0=in0, scalar1=s1, scalar2=None, op0=op0), reads, writes)
        else:
            P.add(eng, lambda e: e.tensor_scalar(out=out, in0=in0, scalar1=s1, scalar2=s2, op0=op0, op1=op1), reads, writes)

    def STT(out, in0, scalar, in1, op0, op1, reads, writes):
        P.add("dve", lambda e: e.scalar_tensor_tensor(out=out, in0=in0, scalar=scalar, in1=in1, op0=op0, op1=op1), reads, writes)

    def CP(eng, out, in_, reads, writes):
        if eng == "act":
            P.add("act", lambda e: e.copy(out=out, in_=in_), reads, writes)
        else:
            P.add(eng, lambda e: e.tensor_copy(out=out, in_=in_), reads, writes)

    def MEMSET(eng, ap, val, writes):
        P.add(eng, lambda e: e.memset(ap, val), [], writes)

    def DMA(q, out, in_, key, reads, writes):
        return P.add(q, lambda e: e.dma_start(out=out, in_=in_), reads, writes, dma_key=key)

    OFF_X = 0
    OFF_GB = 65536
    OFF_MISC = OFF_GB + 8192
    OFF_NS = OFF_MISC + 3072
    OFF_ST = OFF_NS + 1024
    B0 = OFF_ST + 2048
    OFF_HN = B0 + 122880
    OFF_PT = B0 + 126976
    assert OFF_PT + 4096 <= ARENA_BYTES

    X = view(OFF_X, 65536, F32, "p (t d) -> p t d", t=NT)
    rX = [Res("X%d" % t) for t in range(NT)]
    GB = [view(OFF_GB + i * 4096, 4096, F32) for i in range(2)]
    rGB = [Res("GB%d" % i) for i in range(2)]
    ident = view(OFF_MISC + 0, 256, BF16)
    r_ident = Res("ident")
    ones_bf = view(OFF_MISC + 256, 256, BF16)
    r_ones = Res("ones")
    maskc = view(OFF_MISC + 512, 256, BF16)
    r_maskc = Res("maskc")
    gcols = view(OFF_MISC + 768, 384, F32, "p (g k) -> p g k", g=12)
    r_gcols = Res("gcols")
    stats = view(OFF_MISC + 1152, 256, F32)
    r_stats = [Res("stats%d" % i) for i in range(16)]
    lbp = view(OFF_MISC + 1408, 32, F32, "p (l h) -> p l h", l=2)
    omlc = view(OFF_MISC + 1440, 16, F32)
    nomlc = view(OFF_MISC + 1456, 16, F32)
    hgain = view(OFF_MISC + 1472, 8, F32)
    lbtmp = view(OFF_MISC + 1480, 16, F32)
    homl = view(OFF_MISC + 2256, 16, F32)
    nhoml = view(OFF_MISC + 2272, 16, F32)
    bif = view(OFF_MISC + 2288, 16, F32)
    hg05 = view(OFF_MISC + 2304, 4, F32)
    lnhoml = view(OFF_MISC + 2320, 16, F32)
    r_lb = Res("lb")
    expsink = view(OFF_MISC + 1536, 32, F32)
    r_sink = Res("sink")
    kTprev = view(OFF_MISC + 1600, 256, BF16)
    r_kTprev = Res("kTprev")
    Vprev = view(OFF_MISC + 1856, 264, BF16)[:, 0:130].rearrange("p (k d) -> p k d", k=2)
    r_Vprev = Res("Vprev")
    neghalf = view(OFF_MISC + 2248, 4, F32)
    r_neghalf = Res("neghalf")
    notstart = view(OFF_NS, 1024, BF16)
    r_ns = Res("notstart")
    state = view(OFF_ST, 2048, F32, "p (h e) -> p h e", h=4)
    r_state = [Res("state%d" % h) for h in range(4)]
    HN = [view(OFF_HN + i * 2048, 2048, BF16) for i in range(2)]
    r_hn = [Res("hn%d" % i) for i in range(2)]
    PT = view(OFF_PT, 4096, F32)
    r_pt = Res("pt")

    hT = view(B0, 16384, BF16, "p (k t) -> p k t", k=8)
    rhT = [Res("hT%d" % j) for j in range(BT)]

    for t in range(NT):
        DMA("sp", X[:, t, :], x_d[t * 128:(t + 1) * 128, :], ("x", t), [], [rX[t]])
    DMA("pool", ident, ident_d, "c_ident", [], [r_ident])
    DMA("pool", maskc, maskc_d, "c_maskc", [], [r_maskc])
    DMA("pool", notstart, notstart_d, "c_ns", [], [r_ns])
    DMA("sp", gcols, gcols_d, "c_gcols", [], [r_gcols])
    DMA("sp", lbp, lbp_d, "c_lbp", [], [r_lb])
    DMA("sp", hgain, hgain_d, "c_hgain", [], [r_lb])
    MEMSET("dve", ones_bf, 1.0, [r_ones])
    MEMSET("dve", neghalf, -0.5, [r_neghalf])

    tr_bank = [6]

    PT_junk = view(OFF_PT, 2048, BF16)

    def prenorm_stat(b, j, junk=None, rjunk=None):
        tt = b * BT + j
        sl = tt % 16
        sc = stats[:, sl * 4:sl * 4 + 4]
        rs = r_stats[sl]
        if junk is None:
            junk, rjunk = PT_junk, r_pt
        ACT(junk, X[:, tt, :], AF.Square, [rX[tt]], [rjunk, rs], accum_out=sc[:, 0:1])
        TS(sc[:, 1:2], sc[:, 0:1], 1.0 / D, EPS, ALU.mult, ALU.add, [rs], [rs])
        TT(sc[:, 2:3], sc[:, 1:2], neghalf, ALU.pow, [rs, r_neghalf], [rs], eng="pool")

    def prenorm_scale(b, j):
        tt = b * BT + j
        sl = tt % 16
        hn = HN[j % 2]
        rhn = r_hn[j % 2]
        sc = stats[:, sl * 4:sl * 4 + 4]
        rs = r_stats[sl]
        if j % 2 == 0 or (HOIST_STAT and j != 3):
            P.add("act", lambda e, o=hn, i_=X[:, tt, :], m=sc[:, 2:3]: e.mul(out=o, in_=i_, mul=m), [rX[tt], rs], [rhn])
        else:
            TS(hn, X[:, tt, :], sc[:, 2:3], None, ALU.mult, None, [rX[tt], rs], [rhn])

    def prenorm_elem(b, j):
        prenorm_stat(b, j)
        prenorm_scale(b, j)

    def prenorm_block(b, gi, stats_done=False):
        if not stats_done:
            prenorm_stat(b, 0)
            prenorm_stat(b, 1)
        for j in range(BT):
            prenorm_scale(b, j)
            if j + 2 < BT and not stats_done:
                prenorm_stat(b, j + 2)
            prenorm_tr(b, j, gi)

    def prenorm_tr(b, j, gi):
        hn = HN[j % 2]
        rhn = r_hn[j % 2]
        bi = tr_bank[0]
        tr_bank[0] = 6 if bi == 7 else 7
        pb = bankbf(bi).rearrange("p (k t) -> p k t", k=8)
        for k in range(8):
            TR(pb[:, k, :], hn[:, k * 128:(k + 1) * 128], [rhn], [rbank[bi]])
        TT(hT[:, :, j * 128:(j + 1) * 128], pb,
           gcols[:, gi, :].unsqueeze(2).to_broadcast([128, 8, 128]), ALU.mult,
           [rbank[bi], r_gcols], [rhT[j]])

    def postnorm(tt, pi, gbi, fac, eps_mult=1.0):
        sl = tt % 16
        sc = stats[:, sl * 4:sl * 4 + 4]
        rs = r_stats[sl]
        pr = pairs[pi][:, :]
        rpp = [rbank[2 * pi], rbank[2 * pi + 1]]
        ACT(PT, pr, AF.Square, rpp, [r_pt, rs], accum_out=sc[:, 0:1])
        f2 = 1.0 / (fac * fac)
        TS(sc[:, 1:2], sc[:, 0:1], f2 / D, EPS * eps_mult * f2, ALU.mult, ALU.add, [rs], [rs])
        TT(sc[:, 2:3], sc[:, 1:2], neghalf, ALU.pow, [rs, r_neghalf], [rs], eng="pool")
        STT(PT, pr, sc[:, 2:3], GB[gbi], ALU.mult, ALU.mult, rpp + [rs, rGB[gbi]], [r_pt])
        TT(X[:, tt, :], X[:, tt, :], PT, ALU.add, [rX[tt], r_pt], [rX[tt]])

    def load_gain(gi, gbi):
        DMA("sp", GB[gbi], gains_d[gi:gi + 1, :].partition_broadcast(128), ("gb", gbi), [], [rGB[gbi]])

    F_ACT = B0 + 16384
    F_WD = F_ACT + 45056
    F_WGU = F_WD + 45056
    F_SIL = F_WGU + 12288
    assert F_SIL + 4096 == OFF_HN
    actT = view(F_ACT, 45056, BF16, "p (c t) -> p c t", c=NFC)
    WD = view(F_WD, 45056, BF16, "p (c n) -> p c n", c=NFC)
    WGU = [view(F_WGU + s * 4096, 4096, BF16, "p (g k n) -> p g k n", g=2, k=8) for s in range(3)]
    SIL = [view(F_SIL + i * 2048, 2048, F32) for i in range(2)]

    class FFNUnit:
        def __init__(self, l, which, b):
            self.l, self.which, self.b = l, which, b
            self.gi_pre = l * 6 + (0 if which == 0 else 4)
            self.gi_post = self.gi_pre + 1
            self.wg = wg_d[which][l]
            self.wu = wu_d[which][l]
            self.wd = wd_d[which][l]
            self.early_done = False

        def res(self):
            self.r_act = [RG(F_ACT + c * 2048 + n * 1024, 1024, "act%d_%d" % (c, n))[0]
                          for c in range(NFC) for n in range(2)]
            self.r_wd = [RG(F_WD + i * 4096, 4096, "wd%d" % i)[0] for i in range(NFC // 2)]
            self.r_wgu = [RG(F_WGU + s * 4096, 4096, "wg%d" % s, "wu%d" % s) for s in range(3)]
            self.r_sil = [RG(F_SIL + i * 2048, 2048, "sil%d" % i)[0] for i in range(2)]

        def load_gu(self, c):
            s = c % 3
            DMA("pool", WGU[s][:, 0, :, :], self.wg[:, c * 128:(c + 1) * 128].rearrange("(k p) n -> p k n", p=128),
                ("wg", s), [], [self.r_wgu[s][0]])
            DMA("pool", WGU[s][:, 1, :, :], self.wu[:, c * 128:(c + 1) * 128].rearrange("(k p) n -> p k n", p=128),
                ("wu", s), [], [self.r_wgu[s][1]])

        def early(self):
            self.r_wgu = [RG(F_WGU + s * 4096, 4096, "wg%d" % s, "wu%d" % s) for s in range(3)]
            for c in range(3):
                self.load_gu(c)
            self.early_done = True

        def stageA(self):
            if self.b == 0:
                load_gain(self.gi_post, 0)
            if not self.early_done:
                self.early()
            self.res()
            r_act, r_wd, r_wgu, r_sil = self.r_act, self.r_wd, self.r_wgu, self.r_sil
            nwd = 0
            for c in range(NFC):
                s = c % 3
                for n in range(2):
                    st_ = (2 * c + n) % 3
                    bg, bu = 2 * st_, 2 * st_ + 1
                    for k in range(8):
                        MM(bank(bg), WGU[s][:, 0, k, :], hT[:, k, n * 512:(n + 1) * 512], k == 0, k == 7,
                           [r_wgu[s][0]] + rhT[4 * n:4 * n + 4], [rbank[bg]])
                    for k in range(8):
                        MM(bank(bu), WGU[s][:, 1, k, :], hT[:, k, n * 512:(n + 1) * 512], k == 0, k == 7,
                           [r_wgu[s][1]] + rhT[4 * n:4 * n + 4], [rbank[bu]])
                    ACT(SIL[n], bank(bg), AF.Silu, [rbank[bg]], [r_sil[n]])
                    TT(actT[:, c, n * 512:(n + 1) * 512], SIL[n], bank(bu), ALU.mult,
                       [r_sil[n], rbank[bu]], [r_act[c * 2 + n]])
                if nwd < NFC // 2:
                    i = nwd
                    DMA("pool", WD[:, 2 * i:2 * i + 2, :],
                        self.wd[2 * i * 128:(2 * i + 2) * 128, :].rearrange("(c p) n -> p c n", p=128),
                        ("wd", i), [], [r_wd[i]])
                    nwd += 1
                if c + 3 < NFC:
                    self.load_gu(c + 3)

        def stageB(self, j):
            tt = self.b * BT + j
            pi = j % 3
            n = j // 4
            for d2 in range(2):
                for c in range(NFC):
                    MM(pairs[pi][:, d2 * 512:(d2 + 1) * 512], actT[:, c, j * 128:(j + 1) * 128],
                       WD[:, c, d2 * 512:(d2 + 1) * 512], c == 0, c == NFC - 1,
                       [self.r_act[c * 2 + n], self.r_wd[c // 2]], [rbank[2 * pi + d2]])
            postnorm(tt, pi, 0, 0.5)

    M_WA = B0 + 16384
    M_WATT = M_WA + 32768
    M_OGT = M_WATT + 12288
    M_AOT = M_OGT + 8192
    SCR = M_AOT + 8192
    assert SCR + 45056 == OFF_HN
    Whg = view(M_WA, 32768, BF16, "p (h k w n) -> p h k w n", h=4, k=8, w=4)
    Wout = view(M_WA, 16384, BF16, "p (k n) -> p k n", k=8)
    M3S = []
    for s in range(2):
        o = M_WA + 16384 + s * 6144
        M3S.append(dict(
            off=o,
            ga=view(o, 2048, BF16, "p (k n) -> p k n", k=8),
            gb=view(o + 2048, 2048, BF16, "p (k n) -> p k n", k=8),
            bh=view(o + 4096, 1024, BF16, "p (k n) -> p k n", k=4),
            ba=view(o + 5120, 1024, BF16, "p (k n) -> p k n", k=4),
        ))
    Watt = view(M_WATT, 12288, BF16, "p (k n) -> p k n", k=8)
    ogT = view(M_OGT, 8192, BF16, "p (h t) -> p h t", h=4)
    aoT = view(M_AOT, 8192, BF16, "p (h t) -> p h t", h=4)

    def sigmoid_chain3(specs):
        for (T_, src, rsrc, rT_, neg) in specs:
            ACT(T_, src, AF.Exp, rsrc, [rT_], scale=(1.0 if neg else -1.0))
        for (T_, src, rsrc, rT_, neg) in specs:
            ACT(T_, T_, AF.Ln, [rT_], [rT_], bias=1.0)
        for (T_, src, rsrc, rT_, neg) in specs:
            ACT(T_, T_, AF.Exp, [rT_], [rT_], scale=-1.0)

    def mixer_layer_setup(l):
        if l == 0:
            MEMSET("dve", omlc, 1.0, [r_lb])
            MEMSET("dve", nomlc, -1.0, [r_lb])
        else:
            TT(lbtmp, lbp[:, 1, :], lbp[:, 0, :], ALU.subtract, [r_lb], [r_lb])
            ACT(lbtmp, lbtmp, AF.Exp, [r_lb], [r_lb])
            TS(lbtmp, lbtmp, 1.0, None, ALU.add, None, [r_lb], [r_lb])
            P.add("dve", lambda e: e.reciprocal(out=omlc, in_=lbtmp), [r_lb], [r_lb])
            TS(nomlc, omlc, -1.0, None, ALU.mult, None, [r_lb], [r_lb])
        TS(homl, omlc, 0.5, None, ALU.mult, None, [r_lb], [r_lb])
        TS(nhoml, omlc, -0.5, None, ALU.mult, None, [r_lb], [r_lb])
        TS(bif, omlc, -0.5, 1.0, ALU.mult, ALU.add, [r_lb], [r_lb])
        TS(hg05, hgain[:, l:l + 1], 0.5, None, ALU.mult, None, [r_lb], [r_lb])
        ACT(lnhoml, homl, AF.Ln, [r_lb], [r_lb])
        DMA("sp", expsink, sinks_d[l:l + 1, :].partition_broadcast(128), "c_sink", [], [r_sink])
        ACT(expsink, expsink, AF.Exp, [r_sink], [r_sink])
        MEMSET("dve", state.rearrange("p h e -> p (h e)"), 0.0, r_state)
        load_gain(l * 6 + 3, 1)

    class MixUnit:
        def __init__(self, l, b):
            self.l, self.b = l, b
            self.gi_pre = l * 6 + 2
            self.early_done = True

        def early(self):
            pass

        def m1(self):
            l, b = self.l, self.b
            win = win_d[l]
            r_whg = [RG(M_WA + h * 8192, 8192, "whg%d" % h)[0] for h in range(4)]
            for h in range(4):
                DMA("pool", Whg[:, h, :, :, :].rearrange("p k w n -> p k (w n)"),
                    win[:, h * 512:(h + 1) * 512].rearrange("(k p) n -> p k n", p=128), ("whg", h), [], [r_whg[h]])
            self.r_watt = RG(M_WATT, 12288, "watt")[0]
            DMA("pool", Watt, win[:, 2048:2816].rearrange("(k p) n -> p k n", p=128), "watt", [], [self.r_watt])
            self.r_ogT = RG(M_OGT, 8192, *["ogT%d" % j for j in range(BT)])
            r_ogT = self.r_ogT

            Tsets = [[view(SCR + i * 2048, 2048, F32) for i in range(6)]] * 2
            rTsets = [[RG(SCR + i * 2048, 2048, "T%d" % i)[0] for i in range(6)]] * 2
            Dm = view(SCR + 12288, 4096, F32, "p (e c) -> p e c", c=8)
            Zb = view(SCR + 16384, 4096, F32, "p (e c) -> p e c", c=8)
            Zs = view(SCR + 20480, 4096, F32, "p (e c) -> p e c", c=8)
            r_Dm = RG(SCR + 12288, 4096, "Dm")[0]
            r_Zb = RG(SCR + 16384, 4096, "Zb")[0]
            r_Zs = RG(SCR + 20480, 4096, "Zs")[0]
            MEMSET("dve", Dm[:, :, 0:1], 0.0, [r_Dm])
            UB = SCR + 24576

            def ubuf(set_, i):
                return UB + set_ * 5120 + i * 1024
            qtT = [view(ubuf(s_, 0), 1024, BF16) for s_ in range(2)]
            ktT = [view(ubuf(s_, 1), 1024, BF16) for s_ in range(2)]
            kt = [view(ubuf(s_, 2), 1024, BF16, "p (j d) -> p j d", j=4) for s_ in range(2)]
            sgT = [view(ubuf(s_, 3), 1024, BF16) for s_ in range(2)]
            vtm = [view(ubuf(s_, 4), 1024, BF16, "p (j e) -> p j e", j=4) for s_ in range(2)]
            r_u = [[RG(ubuf(s_, i), 1024, "u%d_%d" % (s_, i))[0] for i in range(5)] for s_ in range(2)]
            o2 = SCR + 34816
            scm = [view(o2 + i * 1024, 1024, BF16, "p (j t) -> p j t", j=4) for i in range(2)]
            r_scm = [RG(o2 + i * 1024, 1024, "scm%d" % i)[0] for i in range(2)]
            sqb = view(o2 + 2048, 1024, BF16)
            r_sqb = RG(o2 + 2048, 1024, "sqb")[0]
            Rn = view(o2 + 3072, 2048, F32)
            r_Rn = RG(o2 + 3072, 2048, "Rn")[0]
            otmp = view(o2 + 5120, 2048, F32)
            r_otmp = RG(o2 + 5120, 2048, "otmp")[0]
            decay = [view(o2 + 7168 + i * 32, 32, F32) for i in range(2)]
            r_decay = [RG(o2 + 7168 + i * 32, 32, "decay%d" % i)[0] for i in range(2)]
            SbfA = view(o2 + 7232, 2048, BF16, "p (c e) -> p c e", c=8)
            r_SbfA = RG(o2 + 7232, 2048, "SbfA")[0]
            assert o2 + 7232 + 2048 <= SCR + 45056

            def proj_v(u, h, g):
                tok0 = g * 512
                bv = bank(3).rearrange("p (j e) -> p j e", j=4)
                for jj in range(4):
                    for k in range(8):
                        MM(bv[:, jj, :], hT[:, k, tok0 + jj * 128: tok0 + (jj + 1) * 128], Whg[:, h, k, 2, :],
                           k == 0, k == 7, [rhT[4 * g + jj], r_whg[h]], [rbank[3]])
                    yield

            def proj_qfg(u, h, g):
                tok0 = g * 512
                rh = rhT[4 * g:4 * g + 4]
                n = 0
                for (bi, wi) in ((0, 0), (1, 1), (2, 3)):
                    for k in range(8):
                        MM(bank(bi), Whg[:, h, k, wi, :], hT[:, k, tok0:tok0 + 512], k == 0, k == 7,
                           rh + [r_whg[h]], [rbank[bi]])
                        n += 1
                        if n % 3 == 0:
                            yield

            def elem(u, h, g):
                s_ = u % 2
                T = Tsets[s_]
                rT = rTsets[s_]
                rq, rk, rkt, rsg, rv = r_u[s_]
                bv = bank(3).rearrange("p (j e) -> p j e", j=4)
                CP("act", vtm[s_], bv, [rbank[3]], [rv])
                ACT(T[0], bank(0), AF.Tanh, [rbank[0]], [rT[0]], scale=0.5)
                ACT(T[2], bank(1), AF.Tanh, [rbank[1]], [rT[2]], scale=-0.5)
                ACT(T[3], bank(2), AF.Tanh, [rbank[2]], [rT[3]], scale=0.5)
                ACT(T[4], T[2], AF.Ln, [rT[2], r_lb], [rT[4]], scale=nhoml[:, h:h + 1], bias=bif[:, h:h + 1])
                yield
                STT(T[1], T[0], 1.0, bank(0), ALU.add, ALU.mult, [rT[0], rbank[0]], [rT[1]])
                STT(sgT[s_], T[3], 1.0, bank(2), ALU.add, ALU.mult, [rT[3], rbank[2]], [rsg])
                yield "banks_free"
                P.add("dve", lambda e, o=T[5], d1=T[4]: e.tensor_tensor_scan(
                    out=o, data0=notstart, data1=d1, initial=0.0, op0=ALU.mult, op1=ALU.add),
                    [rT[4], r_ns], [rT[5]])
                yield
                ACT(T[4], T[5], AF.Exp, [rT[5]], [rT[4]])
                ACT(T[0], T[5], AF.Exp, [rT[5], r_lb], [rT[0]], scale=-1.0, bias=lnhoml[:, h:h + 1])
                ACT(decay[s_], T[5].rearrange("p (c t) -> p c t", t=64)[:, :, 63], AF.Exp, [rT[5]], [r_decay[s_]])
                yield
                STT(qtT[s_], T[1], 0.5, T[4], ALU.mult, ALU.mult, [rT[1], rT[4]], [rq])
                yield
                STT(ktT[s_], T[2], 1.0, T[0], ALU.add, ALU.mult, [rT[2], rT[0]], [rk])
                yield
                pb = bankbf(3).rearrange("p (j d) -> p j d", j=8)
                for jj in range(4):
                    TR(pb[:, jj, :], ktT[s_][:, jj * 128:(jj + 1) * 128], [rk], [rbank[3]])
                CP("act", kt[s_], pb[:, 0:4, :], [rbank[3]], [rkt])
                yield

            def rec(u, h, g):
                s_ = u % 2
                rq, rk, rkt, rsg, rv = r_u[s_]
                bs = bank(4).rearrange("p (j t) -> p j t", j=4)
                for jj in range(4):
                    tsl = slice(jj * 128, (jj + 1) * 128)
                    MM(bs[:, jj, :], ktT[s_][:, tsl], qtT[s_][:, tsl], True, True, [rk, rq], [rbank[4]])
                sc_ = scm[s_]
                TT(sc_, bs, maskc.unsqueeze(1).to_broadcast([128, 4, 128]), ALU.mult,
                   [rbank[4], r_maskc], [r_scm[s_]])
                bo = bank(5)
                ci = 0
                for ch in range(2):
                    ub = bank(6 + ch).rearrange("p (c e) -> p c e", c=4)
                    ps = slice(ch * 64, (ch + 1) * 64)
                    for jj in range(4):
                        MM(ub[:, jj, :], kt[s_][ps, jj, :], vtm[s_][ps, jj, :], True, True, [rkt, rv], [rbank[6 + ch]])
                yield
                if RECLVL < 2:
                    return
                dec = decay[s_]
                CP("act", SbfA[:, 0, :], state[:, h, :], [r_state[h]], [r_SbfA])
                CP("dve", Dm[:, :, 1:8], dec[:, 1:8].unsqueeze(1).to_broadcast([128, 128, 7]), [r_decay[s_]], [r_Dm])
                ub0 = bank(6).rearrange("p (c e) -> p c e", c=4)
                TT(ub0[:, 0, :], ub0[:, 0, :], state[:, h, :], ALU.add, [rbank[6], r_state[h]], [rbank[6]])
                Zb_v = Zb.rearrange("p e (jj ch) -> p ch jj e", ch=2)
                for ch in range(2):
                    ub = bank(6 + ch).rearrange("p (c e) -> p c e", c=4)
                    TT(Zb_v[:, ch], ub, dec[:, ch::2].unsqueeze(2).to_broadcast([128, 4, 128]), ALU.mult,
                       [rbank[6 + ch], r_decay[s_]], [r_Zb])
                yield
                P.add("dve", lambda e, o=Zs.rearrange("p e c -> p (e c)"), d0=Dm.rearrange("p e c -> p (e c)"),
                      d1=Zb.rearrange("p e c -> p (e c)"): e.tensor_tensor_scan(
                    out=o, data0=d0, data1=d1, initial=0.0, op0=ALU.mult, op1=ALU.add),
                    [r_Dm, r_Zb], [r_Zs])
                yield
                CP("act", SbfA[:, 1:8, :], Zs.rearrange("p e c -> p c e")[:, 0:7, :], [r_Zs], [r_SbfA])
                CP("dve", state[:, h, :], Zs[:, :, 7], [r_Zs], [r_state[h]])
                yield
                for c in range(8):
                    jj, ch = c // 2, c % 2
                    ps = slice(ch * 64, (ch + 1) * 64)
                    cols = slice(c * 64, (c + 1) * 64)
                    MM(bo[:, cols], vtm[s_][ps, jj, :], sc_[ps, jj, ps], True, False, [rv, r_scm[s_]], [rbank[5]])
                    MM(bo[:, cols], SbfA[:, c, :], qtT[s_][:, cols], False, True, [r_SbfA, rq], [rbank[5]])
                    if c % 2 == 1:
                        yield
                if RECLVL < 3:
                    return
                ACT(sqb, bo, AF.Square, [rbank[5]], [r_sqb])
                MM(bank(4), ones_bf, sqb, True, True, [r_ones, r_sqb], [rbank[4]])
                ACT(Rn, bank(4), AF.Ln, [rbank[4]], [r_Rn], scale=1.0 / 128, bias=EPS)
                ACT(Rn, Rn, AF.Exp, [r_Rn], [r_Rn], scale=-0.5)
                STT(otmp, bo, hg05[:, 0:1], Rn, ALU.mult, ALU.mult, [rbank[5], r_Rn, r_lb], [r_otmp])
                TT(ogT[:, h, g * 512:(g + 1) * 512], otmp, sgT[s_], ALU.mult, [r_otmp, rsg], r_ogT[4 * g:4 * g + 4])
                yield

            units = [(h, g) for h in range(4) for g in range(2)]
            NU = len(units)

            def drain(gen):
                for _ in gen:
                    pass

            def step(gen):
                try:
                    return next(gen), False
                except StopIteration:
                    return None, True

            drain(proj_v(0, *units[0]))
            drain(proj_qfg(0, *units[0]))
            drain(elem(0, *units[0]))
            if NU > 1:
                drain(proj_v(1, *units[1]))
                drain(proj_qfg(1, *units[1]))
            for i in range(NU):
                g_rec = rec(i, *units[i]) if MIXLVL >= 2 else iter(())
                g_el = elem(i + 1, *units[i + 1]) if i + 1 < NU else iter(())
                g_pq = proj_qfg(i + 2, *units[i + 2]) if i + 2 < NU else iter(())
                d_rec = d_el = d_pq = False
                pq_ok = False
                while not (d_rec and d_el and d_pq):
                    if not d_rec:
                        _, d_rec = step(g_rec)
                    if not d_el:
                        tag, d_el = step(g_el)
                        if tag == "banks_free" or d_el:
                            pq_ok = True
                    if i + 1 >= NU:
                        pq_ok = True
                    if pq_ok and not d_pq:
                        _, d_pq = step(g_pq)
                        if not d_pq:
                            _, d_pq = step(g_pq)
                if i + 2 < NU:
                    drain(proj_v(i + 2, *units[i + 2]))

        def m2(self):
            l, b = self.l, self.b
            r_watt = self.r_watt
            self.r_aoT = RG(M_AOT, 8192, *["aoT%d" % j for j in range(BT)])
            r_aoT = self.r_aoT
            qT = view(SCR + 0, 8192, BF16, "p (c t) -> p c t", c=4)
            kT = view(SCR + 8192, 2304, BF16)
            Vaug = view(SCR + 10752, 2560, BF16)[:, 0:1170].rearrange("p (j k d) -> p j k d", j=9, k=2)
            abst = view(SCR + 13312, 8192, F32)
            expb = view(SCR + 21504, 4096, BF16, "p (b h q) -> p b h q", b=2, h=8)
            etmp = [view(SCR + 25600 + i * 2048, 1024, BF16) for i in range(2)]
            pT = [[[view(SCR + 29696 + ((s_ * 2 + kv) * 2 + kb) * 1024, 1024, BF16) for kb in range(2)]
                   for kv in range(2)] for s_ in range(2)]
            ao = [view(SCR + 37888 + i * 1024, 1024, BF16) for i in range(2)]
            den = view(SCR + 39936, 128, F32)
            r_qT = RG(SCR + 0, 8192, "qT0", "qT1")
            r_kT = RG(SCR + 8192, 2304, "kTp", "kT0", "kT1")
            r_V = RG(SCR + 10752, 2560, *["V%d" % j_ for j_ in range(9)])
            r_abst = RG(SCR + 13312, 8192, "abst")[0]
            r_expb = RG(SCR + 21504, 4096, "expb")[0]
            r_etmp = [RG(SCR + 25600 + i * 2048, 2048, "etmp%d" % i)[0] for i in range(2)]
            r_pT = [[[RG(SCR + 29696 + ((s_ * 2 + kv) * 2 + kb) * 1024, 1024, "pT%d%d%d" % (s_, kv, kb))[0]
                      for kb in range(2)] for kv in range(2)] for s_ in range(2)]
            r_ao = [RG(SCR + 37888 + i * 1024, 1024, "ao%d" % i)[0] for i in range(2)]
            r_den = RG(SCR + 39936, 128, "den")[0]

            DMA("sp", abst, abias_d, "abias", [], [r_abst])
            ACT(expb.rearrange("p b h q -> p (b h q)"), abst, AF.Exp, [r_abst], [r_expb])
            MEMSET("dve", Vaug[:, :, :, 64], 1.0, r_V)
            if b > 0:
                CP("dve", kT[:, 0:128], kTprev, [r_kTprev], [r_kT[0]])
                CP("dve", Vaug[:, 0, :, :], Vprev, [r_Vprev], [r_V[0]])
            pbi = 6
            for g in range(2):
                rh = rhT[4 * g:4 * g + 4]
                for cc in range(4):
                    for k in range(8):
                        MM(bank(pbi), Watt[:, k, cc * 128:(cc + 1) * 128], hT[:, k, g * 512:(g + 1) * 512],
                           k == 0, k == 7, rh + [r_watt], [rbank[pbi]])
                    CP("act" if cc % 2 == 0 else "dve", qT[:, cc, g * 512:(g + 1) * 512], bank(pbi),
                       [rbank[pbi]], [r_qT[g]])
                    pbi = 13 - pbi
                for k in range(8):
                    MM(bank(pbi), Watt[:, k, 512:640], hT[:, k, g * 512:(g + 1) * 512],
                       k == 0, k == 7, rh + [r_watt], [rbank[pbi]])
                CP("act", kT[:, 128 + g * 512: 128 + (g + 1) * 512], bank(pbi), [rbank[pbi]], [r_kT[1 + g]])
                pbi = 13 - pbi
                bv = bank(pbi).rearrange("p (j n) -> p j n", j=4)
                for jj in range(4):
                    for k in range(8):
                        MM(bv[:, jj, :], hT[:, k, g * 512 + jj * 128: g * 512 + (jj + 1) * 128], Watt[:, k, 640:768],
                           k == 0, k == 7, [rhT[4 * g + jj], r_watt], [rbank[pbi]])
                P.add("dve", lambda e, o=Vaug[:, 1 + 4 * g:5 + 4 * g, :, 0:64],
                      i_=bv.rearrange("p j (k d) -> p j k d", k=2): e.tensor_copy(out=o, in_=i_),
                      [rbank[pbi]], r_V[1 + 4 * g:5 + 4 * g])
                pbi = 13 - pbi
            if b == 0:
                CP("dve", kTprev, kT[:, 1024:1152], [r_kT[2]], [r_kTprev])
                CP("dve", Vprev, Vaug[:, 8, :, :], [r_V[8]], [r_Vprev])
            def att_stage1(j):
                tt = b * BT + j
                s_ = j % 2
                kbs = [1] if tt == 0 else [0, 1]
                for kv in range(2):
                    prt = slice(kv * 64, (kv + 1) * 64)
                    for kb in kbs:
                        bi = (kv * 2 + kb)
                        kc0 = (j + kb) * 128
                        MM(bank(bi), kT[prt, kc0:kc0 + 128], qT[prt, :, j * 128:(j + 1) * 128], True, True,
                           r_kT + [r_qT[j // 4]], [rbank[bi]])
                        et = etmp[(kv * 2 + kb) % 2]
                        ret = r_etmp[(kv * 2 + kb) % 2]
                        ACT(et, bank(bi), AF.Exp, [rbank[bi]], [ret], scale=0.125)
                        TT(pT[s_][kv][kb].rearrange("p (h q) -> p h q", h=4), et.rearrange("p (h q) -> p h q", h=4),
                           expb[:, kb, kv * 4:(kv + 1) * 4, :], ALU.mult, [ret, r_expb], [r_pT[s_][kv][kb]])

            def att_stage2(j):
                tt = b * BT + j
                s_ = j % 2
                kbs = [1] if tt == 0 else [0, 1]
                for kv in range(2):
                    bpv = bank(4 + kv)[:, 0:260].rearrange("p (h d) -> p h d", h=4)
                    for g4 in range(4):
                        for n_, kb in enumerate(kbs):
                            MM(bpv[:, g4, :], pT[s_][kv][kb][:, g4 * 128:(g4 + 1) * 128], Vaug[:, j + kb, kv, :],
                               n_ == 0, n_ == len(kbs) - 1, [r_pT[s_][kv][kb], r_V[j + kb]], [rbank[4 + kv]])
                    dn = den[:, kv * 8:kv * 8 + 4]
                    rd = den[:, kv * 8 + 4:kv * 8 + 8]
                    TT(dn, bpv[:, :, 64], expsink[:, kv * 4:(kv + 1) * 4], ALU.add, [rbank[4 + kv], r_sink], [r_den])
                    P.add("dve", lambda e, o=rd, i_=dn: e.reciprocal(out=o, in_=i_), [r_den], [r_den])
                    TT(ao[s_][:, kv * 256:(kv + 1) * 256].rearrange("p (h d) -> p h d", h=4), bpv[:, :, 0:64],
                       rd.unsqueeze(2).to_broadcast([128, 4, 64]), ALU.mult, [rbank[4 + kv], r_den], [r_ao[s_]])
                tb = 6 + (j % 2)
                pb = bankbf(tb).rearrange("p (c t) -> p c t", c=8)
                for cc in range(4):
                    TR(pb[:, cc, :], ao[s_][:, cc * 128:(cc + 1) * 128], [r_ao[s_]], [rbank[tb]])
                CP("act", aoT[:, :, j * 128:(j + 1) * 128], pb[:, 0:4, :], [rbank[tb]], [r_aoT[j]])


            att_stage1(0)
            for j in range(BT):
                if j + 1 < BT:
                    att_stage1(j + 1)
                att_stage2(j)

        def m3(self):
            l, b = self.l, self.b
            win = win_d[l]
            r_ogT, r_aoT = self.r_ogT, self.r_aoT
            self.mixT = view(SCR + 0, 16384, BF16, "p (c t) -> p c t", c=8)
            mixT = self.mixT
            self.r_mixT = [RG(SCR + c_ * 2048 + g_ * 1024, 1024, "mixT%d_%d" % (c_, g_))[0]
                           for c_ in range(8) for g_ in range(2)]
            r_mixT = self.r_mixT
            TM = [[view(SCR + 16384 + (s2 * 4 + i) * 2048, 2048, F32) for i in range(4)] for s2 in range(2)]
            r_TM = [[RG(SCR + 16384 + (s2 * 4 + i) * 2048, 2048, "TM%d_%d" % (s2, i))[0] for i in range(4)]
                    for s2 in range(2)]
            self.r_wout = [RG(M_WA + i * 8192, 8192, "wout%d" % i)[0] for i in range(2)]
            for i in range(2):
                DMA("pool", Wout[:, 4 * i:4 * i + 4, :],
                    wout_d[l][i * 512:(i + 1) * 512, :].rearrange("(k p) n -> p k n", p=128), ("wout", i), [], [self.r_wout[i]])
            r_m3s = [dict(zip(("ga", "gb", "bh", "ba"),
                              [RG(M3S[s]["off"], 2048, "ga%d" % s)[0], RG(M3S[s]["off"] + 2048, 2048, "gb%d" % s)[0],
                               RG(M3S[s]["off"] + 4096, 1024, "bh%d" % s)[0], RG(M3S[s]["off"] + 5120, 1024, "ba%d" % s)[0]]))
                     for s in range(2)]

            def load_m3(c):
                s = c % 2
                DMA("pool", M3S[s]["ga"], win[:, 2816 + c * 128: 2816 + (c + 1) * 128].rearrange("(k p) n -> p k n", p=128),
                    ("ga", s), [], [r_m3s[s]["ga"]])
                DMA("pool", M3S[s]["gb"], win[:, 3840 + c * 128: 3840 + (c + 1) * 128].rearrange("(k p) n -> p k n", p=128),
                    ("gb", s), [], [r_m3s[s]["gb"]])
                DMA("pool", M3S[s]["bh"], wbh_d[l][:, c * 128:(c + 1) * 128].rearrange("(k p) n -> p k n", p=128),
                    ("bh", s), [], [r_m3s[s]["bh"]])
                DMA("pool", M3S[s]["ba"], wba_d[l][:, c * 128:(c + 1) * 128].rearrange("(k p) n -> p k n", p=128),
                    ("ba", s), [], [r_m3s[s]["ba"]])

            load_m3(0)
            load_m3(1)
            it = 0
            for c in range(8):
                s = c % 2
                for g in range(2):
                    s2 = it % 2
                    it += 1
                    b4 = 4 * s2
                    rh = rhT[4 * g:4 * g + 4]
                    tsl = slice(g * 512, (g + 1) * 512)
                    for k in range(8):
                        MM(bank(b4), M3S[s]["ga"][:, k, :], hT[:, k, tsl], k == 0, k == 7, rh + [r_m3s[s]["ga"]], [rbank[b4]])
                    for k in range(8):
                        MM(bank(b4 + 1), M3S[s]["gb"][:, k, :], hT[:, k, tsl], k == 0, k == 7, rh + [r_m3s[s]["gb"]], [rbank[b4 + 1]])
                    for k in range(4):
                        MM(bank(b4 + 2), M3S[s]["bh"][:, k, :], ogT[:, k, tsl], k == 0, k == 3,
                           r_ogT[4 * g:4 * g + 4] + [r_m3s[s]["bh"]], [rbank[b4 + 2]])
                    for k in range(4):
                        MM(bank(b4 + 3), M3S[s]["ba"][:, k, :], aoT[:, k, tsl], k == 0, k == 3,
                           r_aoT[4 * g:4 * g + 4] + [r_m3s[s]["ba"]], [rbank[b4 + 3]])
                    Ta, Tb, U1, U2 = TM[s2]
                    rTa, rTb, rU1, rU2 = r_TM[s2]
                    ACT(Ta, bank(b4), AF.Tanh, [rbank[b4]], [rTa], scale=0.5)
                    ACT(Tb, bank(b4 + 1), AF.Tanh, [rbank[b4 + 1]], [rTb], scale=0.5)
                    STT(U1, Ta, 1.0, bank(b4 + 2), ALU.add, ALU.mult, [rTa, rbank[b4 + 2]], [rU1])
                    STT(U2, Tb, 1.0, bank(b4 + 3), ALU.add, ALU.mult, [rTb, rbank[b4 + 3]], [rU2])
                    TT(mixT[:, c, tsl], U1, U2, ALU.add, [rU1, rU2], [r_mixT[c * 2 + g]])
                if c + 2 < 8:
                    load_m3(c + 2)

        def stageA(self):
            if self.b == 0:
                mixer_layer_setup(self.l)
            self.m1()
            if MIXLVL >= 3:
                self.m2()
            if MIXLVL >= 4:
                self.m3()

        def stageB(self, j):
            if MIXLVL < 4:
                return
            tt = self.b * BT + j
            pi = j % 3
            g = j // 4
            for d2 in range(2):
                for c in range(8):
                    MM(pairs[pi][:, d2 * 512:(d2 + 1) * 512], self.mixT[:, c, j * 128:(j + 1) * 128],
                       Wout[:, c, d2 * 512:(d2 + 1) * 512], c == 0, c == 7,
                       [self.r_mixT[c * 2 + g], self.r_wout[c // 4]], [rbank[2 * pi + d2]])
            postnorm(tt, pi, 1, 1.0, eps_mult=4.0)

    units = []
    for l in range(DEPTH):
        names = ["L%d_0" % l, "L%d_1" % l, "L%d_2" % l]
        units += [FFNUnit(l, 0, 0), FFNUnit(l, 0, 1)]
        if stop_after == names[0]:
            break
        units += [MixUnit(l, 0), MixUnit(l, 1)]
        if stop_after == names[1]:
            break
        units += [FFNUnit(l, 1, 0), FFNUnit(l, 1, 1)]
        if stop_after == names[2]:
            break

    u0 = units[0]
    prenorm_block(u0.b, u0.gi_pre)
    for ui, u in enumerate(units):
        nxt = units[ui + 1] if ui + 1 < len(units) else None
        u.stageA()
        if nxt is not None and HOIST_PRE:
            prenorm_elem(nxt.b, 0)
        for j in range(BT):
            if nxt is not None and HOIST_PRE:
                if j + 1 < BT:
                    prenorm_elem(nxt.b, j + 1)
                prenorm_tr(nxt.b, j, nxt.gi_pre)
            u.stageB(j)
            if HOIST_STAT and nxt is not None:
                prenorm_stat(nxt.b, j, HN[0], r_hn[0])
            if HOIST_EARLY and nxt is not None and j == 3 and isinstance(nxt, FFNUnit):
                nxt.early()

        if nxt is not None and not HOIST_PRE:
            prenorm_block(nxt.b, nxt.gi_pre, stats_done=HOIST_STAT)

    fin = []
    for t in range(NT):
        fin.append(DMA("sp", out_d[t * 128:(t + 1) * 128, :], X[:, t, :], ("x", t), [rX[t]], []))
    P.emit(nc, final_waits=fin)
    st.close()
    return nc


def _consts():
    ident = np.eye(128, dtype=np.float32)
    s = np.arange(128)[:, None]
    t = np.arange(128)[None, :]
    maskc = ((s // 64 == t // 64) & (s <= t)).astype(np.float32)
    notstart = np.ones((128, 512), np.float32)
    notstart[:, ::64] = 0.0
    slopes = np.exp2(-8.0 * np.arange(1, 9, dtype=np.float32) / 8.0)
    k = np.arange(128)[:, None, None]
    q = np.arange(128)[None, None, :]
    ab = np.empty((128, 2, 8, 128), np.float32)
    for kb in range(2):
        dist = (q - k + (128 if kb == 0 else 0)).astype(np.float32)
        valid = (k > q) if kb == 0 else (k <= q)
        ab[:, kb] = np.where(valid, -slopes[None, :, None] * dist, -30000.0)
    return ident, maskc, notstart, ab.reshape(128, -1)


_PROG_CACHE = {}


def kernel(x, norm_gains, w_ffn1_gate, w_ffn1_up, w_ffn1_down, w_in, hgrn_lower_bounds,
           hgrn_head_gain, attn_sinks, w_branch_hgrn, w_branch_attn, w_out,
           w_ffn2_gate, w_ffn2_up, w_ffn2_down, _stop_after=None):
    f = lambda a: np.ascontiguousarray(np.asarray(a, dtype=np.float32))
    x = f(x)
    ng = f(norm_gains)
    gains = ng.reshape(12, D)
    gcols = np.ascontiguousarray(ng.reshape(12, 8, 128).transpose(2, 0, 1))
    w_in = f(w_in).copy()
    perm = []
    for c in range(4):
        perm += list(range(2048 + c * 64, 2048 + (c + 1) * 64))
        perm += list(range(2048 + (4 + c) * 64, 2048 + (5 + c) * 64))
    w_in[:, :, 2048:2560] = w_in[:, :, perm]
    permh = [w_ * 512 + h_ * 128 + n_ for h_ in range(4) for w_ in range(4) for n_ in range(128)]
    w_in[:, :, 0:2048] = w_in[:, :, permh]
    lbp = np.ascontiguousarray(f(hgrn_lower_bounds).reshape(2, 4, 128).transpose(2, 0, 1))
    hgain = np.ascontiguousarray(f(hgrn_head_gain).T)
    ident, maskc, notstart, abias = _consts()
    shared = {
        "gcols": gcols, "gains": gains,
        "wg1": f(w_ffn1_gate), "wu1": f(w_ffn1_up), "wd1": f(w_ffn1_down),
        "wg2": f(w_ffn2_gate), "wu2": f(w_ffn2_up), "wd2": f(w_ffn2_down),
        "w_in": w_in, "w_bh": f(w_branch_hgrn), "w_ba": f(w_branch_attn), "w_out": f(w_out),
        "lbp": lbp, "hgain": hgain, "sinks": f(attn_sinks),
        "ident": ident, "maskc": maskc, "notstart": notstart, "abias": abias,
    }
    key = _stop_after
    if key not in _PROG_CACHE:
        _PROG_CACHE[key] = build_program(_stop_after)
    nc = _PROG_CACHE[key]
    in_maps = [dict(shared, x=x[i]) for i in range(8)]
    res = run_bass_kernel_spmd(nc, in_maps, core_ids=list(range(8)))
    return np.stack([np.asarray(r["out"], dtype=np.float32) for r in res.results], axis=0)
```

```python
import numpy as np
from contextlib import ExitStack

import concourse.bass as bass
import concourse.mybir as mybir
from concourse.bass_utils import run_bass_kernel_spmd

F32 = mybir.dt.float32
BF16 = mybir.dt.bfloat16
AF = mybir.ActivationFunctionType
ALU = mybir.AluOpType

ENGS = ("pe", "act", "dve", "pool", "sp")

S = 2048
D = 1024
DFF = 2816
NFC = DFF // 128
DIN = 4864
DEPTH = 2
EPS = 1e-6
NT = S // 128
BT = 8
NB = NT // BT
BTOK = BT * 128


class Res:
    __slots__ = ("name", "lw", "rd", "rd_dma", "inherit")

    def __init__(self, name):
        self.name = name
        self.lw = None
        self.rd = {}
        self.rd_dma = []
        self.inherit = []


class Op:
    __slots__ = ("eng", "fn", "deps", "dma_key", "signal", "tick", "waits")

    def __init__(self, eng, fn, dma_key):
        self.eng = eng
        self.fn = fn
        self.deps = []
        self.dma_key = dma_key
        self.signal = False
        self.tick = 0
        self.waits = None


class Prog:
    def __init__(self):
        self.ops = []
        self.pending_bar = {e: [] for e in ENGS}
        self.last_on = {}
        self.dma_last = {}
        self.live = []

    def region(self, off, nbytes, names):
        s_, e_ = off, off + nbytes
        names = tuple(names)
        inherit = []
        keep = []
        for ent in self.live:
            a, b, nm, rs = ent
            if a < e_ and s_ < b:
                if a == s_ and b == e_ and nm == names:
                    return rs
                for r in rs:
                    if r.lw is not None:
                        inherit.append(r.lw)
                    inherit.extend(r.rd.values())
                    inherit.extend(r.rd_dma)
                    inherit.extend(r.inherit)
            else:
                keep.append(ent)
        inh = sorted(set(inherit))
        rs = []
        for n in names:
            r = Res(n)
            r.inherit = list(inh)
            rs.append(r)
        keep.append((s_, e_, names, rs))
        self.live = keep
        return rs

    def add(self, eng, fn, reads=(), writes=(), dma_key=None):
        i = len(self.ops)
        op = Op(eng, fn, dma_key)
        deps = {}

        def dep(j, kind):
            if j is None or j == i:
                return
            k = deps.get(j)
            if k is None or kind == "raw":
                deps[j] = kind

        for r in reads:
            dep(r.lw, "raw")
            for j in r.inherit:
                dep(j, "raw")
        for w in writes:
            for j in w.inherit:
                dep(j, "raw")
            w.inherit = []
            dep(w.lw, "waw")
            for j in w.rd.values():
                dep(j, "war")
            for j in w.rd_dma:
                dep(j, "war")
        for j in self.pending_bar[eng]:
            dep(j, "raw")
        self.pending_bar[eng] = []
        wset = set(id(w) for w in writes)
        for w in writes:
            w.lw = i
            w.rd = {}
            w.rd_dma = []
        for r in reads:
            if id(r) in wset:
                continue
            if dma_key is not None:
                r.rd_dma.append(i)
            else:
                r.rd[eng] = i
        op.deps = list(deps.items())
        self.ops.append(op)
        if dma_key is None:
            self.last_on[eng] = i
        else:
            self.dma_last[dma_key] = i
        return i

    def barrier(self):
        lst = list(self.last_on.values()) + list(self.dma_last.values())
        for e in ENGS:
            self.pending_bar[e] = list(lst)

    def finalize(self):
        ops = self.ops
        for i, op in enumerate(ops):
            keep = []
            for j, kind in op.deps:
                pj = ops[j]
                if pj.dma_key is not None:
                    keep.append(j)
                elif pj.eng == op.eng:
                    if op.dma_key is not None:
                        keep.append(j)
                    elif op.eng == "pe":
                        continue
                    elif kind == "raw":
                        keep.append(j)
                else:
                    keep.append(j)
            op.deps = keep
            for j in keep:
                ops[j].signal = True
        cnt = {}
        for op in ops:
            if op.dma_key is not None:
                k = ("dma", op.dma_key)
                cnt[k] = cnt.get(k, 0) + 16
                op.tick = cnt[k]
            elif op.signal:
                k = ("eng", op.eng)
                cnt[k] = cnt.get(k, 0) + 1
                op.tick = cnt[k]
        self.sem_keys = list(cnt.keys())
        seen = {e: {} for e in ENGS}
        for op in ops:
            need = {}
            s = seen[op.eng]
            for j in op.deps:
                pj = ops[j]
                k = ("dma", pj.dma_key) if pj.dma_key is not None else ("eng", pj.eng)
                if s.get(k, 0) >= pj.tick:
                    continue
                if need.get(k, 0) < pj.tick:
                    need[k] = pj.tick
            for k, v in need.items():
                s[k] = v
            op.waits = list(need.items())

    def emit(self, nc, final_waits=()):
        self.finalize()
        ops = self.ops
        with ExitStack() as st:
            sems = {}
            for n, k in enumerate(self.sem_keys):
                sems[k] = st.enter_context(nc.semaphore("sem%d" % n))
            block = st.enter_context(nc.Block())

            def run(engname):
                def body(e):
                    for op in ops:
                        if op.eng != engname:
                            continue
                        for k, v in op.waits:
                            e.wait_ge(sems[k], v)
                        ins = op.fn(e)
                        if op.dma_key is not None:
                            ins.then_inc(sems[("dma", op.dma_key)], 16)
                        elif op.signal:
                            ins.then_inc(sems[("eng", op.eng)], 1)
                    if engname == "sp":
                        for j in final_waits:
                            pj = ops[j]
                            e.wait_ge(sems[("dma", pj.dma_key)], pj.tick)
                return body

            block.tensor(run("pe"))
            block.scalar(run("act"))
            block.vector(run("dve"))
            block.gpsimd(run("pool"))
            block.sync(run("sp"))


ARENA_BYTES = 211968
import os as _os
HOIST_EARLY = _os.environ.get('K_HOIST_EARLY', '1') == '1'
HOIST_PRE = _os.environ.get('K_HOIST_PRE', '0') == '1'
MIXLVL = int(_os.environ.get('K_MIX', '4'))
HOIST_STAT = _os.environ.get('K_HOIST_STAT', '1') == '1'
RECLVL = int(_os.environ.get('K_REC', '9'))
ILV = _os.environ.get('K_ILV', '1') == '1'


def build_program(stop_after=None):
    nc = bass.Bass("TRN2", target_bir_lowering=False)

    def din(name, shape):
        return nc.dram_tensor(name, list(shape), F32, kind="ExternalInput").ap()

    x_d = din("x", [S, D])
    out_d = nc.dram_tensor("out", [S, D], F32, kind="ExternalOutput").ap()
    gcols_d = din("gcols", [128, 12, 8])
    gains_d = din("gains", [12, D])
    wg_d = [din("wg1", [DEPTH, D, DFF]), din("wg2", [DEPTH, D, DFF])]
    wu_d = [din("wu1", [DEPTH, D, DFF]), din("wu2", [DEPTH, D, DFF])]
    wd_d = [din("wd1", [DEPTH, DFF, D]), din("wd2", [DEPTH, DFF, D])]
    win_d = din("w_in", [DEPTH, D, DIN])
    wbh_d = din("w_bh", [DEPTH, 512, D])
    wba_d = din("w_ba", [DEPTH, 512, D])
    wout_d = din("w_out", [DEPTH, D, D])
    lbp_d = din("lbp", [128, 2, 4])
    hgain_d = din("hgain", [128, 2])
    sinks_d = din("sinks", [DEPTH, 8])
    ident_d = din("ident", [128, 128])
    maskc_d = din("maskc", [128, 128])
    notstart_d = din("notstart", [128, 512])
    abias_d = din("abias", [128, 2 * 8 * 128])

    P = Prog()
    st = ExitStack()
    arena = st.enter_context(nc.sbuf_tensor("arena", [128, ARENA_BYTES // 2], BF16))
    pairs = [st.enter_context(nc.psum_tensor("pp%d" % i, [128, 1024], F32)) for i in range(4)]

    def bank(i):
        return pairs[i // 2][:, (i % 2) * 512:(i % 2) * 512 + 512]

    def bankbf(i):
        return bank(i).bitcast(BF16)

    rbank = [Res("bank%d" % i) for i in range(8)]

    def view(off, nbytes, dt, pat=None, **kw):
        assert off % 4 == 0 and off + nbytes <= ARENA_BYTES, (off, nbytes)
        v = arena[:, off // 2:(off + nbytes) // 2]
        if dt == F32:
            v = v.bitcast(F32)
        if pat:
            v = v.rearrange(pat, **kw)
        return v

    def RG(off, nbytes, *names):
        return P.region(off, nbytes, names)

    def MM(out, lhsT, rhs, start, stop, reads, writes):
        P.add("pe", lambda e: e.matmul(out, lhsT=lhsT, rhs=rhs, start=start, stop=stop), reads, writes)

    def TR(out, in_, reads, writes):
        P.add("pe", lambda e: e.transpose(out, in_, ident), list(reads) + [r_ident], writes)

    def ACT(out, in_, func, reads, writes, **kw):
        P.add("act", lambda e: e.activation(out=out, in_=in_, func=func, **kw), reads, writes)

    def TT(out, in0, in1, op, reads, writes, eng="dve"):
        P.add(eng, lambda e: e.tensor_tensor(out=out, in0=in0, in1=in1, op=op), reads, writes)

    def TS(out, in0, s1, s2, op0, op1, reads, writes, eng="dve"):
        if s2 is None:
            P.add(eng, lambda e: e.tensor_scalar(out=out, in0=in0, scalar1=s1, scalar2=None, op0=op0), reads, writes)
        else:
            P.add(eng, lambda e: e.tensor_scalar(out=out, in0=in0, scalar1=s1, scalar2=s2, op0=op0, op1=op1), reads, writes)

    def STT(out, in0, scalar, in1, op0, op1, reads, writes):
        P.add("dve", lambda e: e.scalar_tensor_tensor(out=out, in0=in0, scalar=scalar, in1=in1, op0=op0, op1=op1), reads, writes)

    def CP(eng, out, in_, reads, writes):
        if eng == "act":
            P.add("act", lambda e: e.copy(out=out, in_=in_), reads, writes)
        else:
            P.add(eng, lambda e: e.tensor_copy(out=out, in_=in_), reads, writes)

    def MEMSET(eng, ap, val, writes):
        P.add(eng, lambda e: e.memset(ap, val), [], writes)

    def DMA(q, out, in_, key, reads, writes):
        return P.add(q, lambda e: e.dma_start(out=out, in_=in_), reads, writes, dma_key=key)

    OFF_X = 0
    OFF_GB = 65536
    OFF_MISC = OFF_GB + 8192
    OFF_NS = OFF_MISC + 3072
    OFF_ST = OFF_NS + 1024
    B0 = OFF_ST + 2048
    OFF_HN = B0 + 122880
    OFF_PT = B0 + 126976
    assert OFF_PT + 4096 <= ARENA_BYTES

    X = view(OFF_X, 65536, F32, "p (t d) -> p t d", t=NT)
    rX = [Res("X%d" % t) for t in range(NT)]
    GB = [view(OFF_GB + i * 4096, 4096, F32) for i in range(2)]
    rGB = [Res("GB%d" % i) for i in range(2)]
    ident = view(OFF_MISC + 0, 256, BF16)
    r_ident = Res("ident")
    ones_bf = view(OFF_MISC + 256, 256, BF16)
    r_ones = Res("ones")
    maskc = view(OFF_MISC + 512, 256, BF16)
    r_maskc = Res("maskc")
    gcols = view(OFF_MISC + 768, 384, F32, "p (g k) -> p g k", g=12)
    r_gcols = Res("gcols")
    stats = view(OFF_MISC + 1152, 256, F32)
    r_stats = [Res("stats%d" % i) for i in range(16)]
    lbp = view(OFF_MISC + 1408, 32, F32, "p (l h) -> p l h", l=2)
    omlc = view(OFF_MISC + 1440, 16, F32)
    nomlc = view(OFF_MISC + 1456, 16, F32)
    hgain = view(OFF_MISC + 1472, 8, F32)
    lbtmp = view(OFF_MISC + 1480, 16, F32)
    homl = view(OFF_MISC + 2256, 16, F32)
    nhoml = view(OFF_MISC + 2272, 16, F32)
    bif = view(OFF_MISC + 2288, 16, F32)
    hg05 = view(OFF_MISC + 2304, 4, F32)
    lnhoml = view(OFF_MISC + 2320, 16, F32)
    r_lb = Res("lb")
    expsink = view(OFF_MISC + 1536, 32, F32)
    r_sink = Res("sink")
    kTprev = view(OFF_MISC + 1600, 256, BF16)
    r_kTprev = Res("kTprev")
    Vprev = view(OFF_MISC + 1856, 264, BF16)[:, 0:130].rearrange("p (k d) -> p k d", k=2)
    r_Vprev = Res("Vprev")
    neghalf = view(OFF_MISC + 2248, 4, F32)
    r_neghalf = Res("neghalf")
    notstart = view(OFF_NS, 1024, BF16)
    r_ns = Res("notstart")
    state = view(OFF_ST, 2048, F32, "p (h e) -> p h e", h=4)
    r_state = [Res("state%d" % h) for h in range(4)]
    HN = [view(OFF_HN + i * 2048, 2048, BF16) for i in range(2)]
    r_hn = [Res("hn%d" % i) for i in range(2)]
    PT = view(OFF_PT, 4096, F32)
    r_pt = Res("pt")

    hT = view(B0, 16384, BF16, "p (k t) -> p k t", k=8)
    rhT = [Res("hT%d" % j) for j in range(BT)]

    for t in range(NT):
        DMA("sp", X[:, t, :], x_d[t * 128:(t + 1) * 128, :], ("x", t), [], [rX[t]])
    DMA("pool", ident, ident_d, "c_ident", [], [r_ident])
    DMA("pool", maskc, maskc_d, "c_maskc", [], [r_maskc])
    DMA("pool", notstart, notstart_d, "c_ns", [], [r_ns])
    DMA("sp", gcols, gcols_d, "c_gcols", [], [r_gcols])
    DMA("sp", lbp, lbp_d, "c_lbp", [], [r_lb])
    DMA("sp", hgain, hgain_d, "c_hgain", [], [r_lb])
    MEMSET("dve", ones_bf, 1.0, [r_ones])
    MEMSET("dve", neghalf, -0.5, [r_neghalf])

    tr_bank = [6]

    PT_junk = view(OFF_PT, 2048, BF16)

    def prenorm_stat(b, j, junk=None, rjunk=None):
        tt = b * BT + j
        sl = tt % 16
        sc = stats[:, sl * 4:sl * 4 + 4]
        rs = r_stats[sl]
        if junk is None:
            junk, rjunk = PT_junk, r_pt
        ACT(junk, X[:, tt, :], AF.Square, [rX[tt]], [rjunk, rs], accum_out=sc[:, 0:1])
        TS(sc[:, 1:2], sc[:, 0:1], 1.0 / D, EPS, ALU.mult, ALU.add, [rs], [rs])
        TT(sc[:, 2:3], sc[:, 1:2], neghalf, ALU.pow, [rs, r_neghalf], [rs], eng="pool")

    def prenorm_scale(b, j):
        tt = b * BT + j
        sl = tt % 16
        hn = HN[j % 2]
        rhn = r_hn[j % 2]
        sc = stats[:, sl * 4:sl * 4 + 4]
        rs = r_stats[sl]
        if j % 2 == 0 or HOIST_STAT:
            P.add("act", lambda e, o=hn, i_=X[:, tt, :], m=sc[:, 2:3]: e.mul(out=o, in_=i_, mul=m), [rX[tt], rs], [rhn])
        else:
            TS(hn, X[:, tt, :], sc[:, 2:3], None, ALU.mult, None, [rX[tt], rs], [rhn])

    def prenorm_elem(b, j):
        prenorm_stat(b, j)
        prenorm_scale(b, j)

    def prenorm_block(b, gi, stats_done=False):
        if not stats_done:
            prenorm_stat(b, 0)
            prenorm_stat(b, 1)
        for j in range(BT):
            prenorm_scale(b, j)
            if j + 2 < BT and not stats_done:
                prenorm_stat(b, j + 2)
            prenorm_tr(b, j, gi)

    def prenorm_tr(b, j, gi):
        hn = HN[j % 2]
        rhn = r_hn[j % 2]
        bi = tr_bank[0]
        tr_bank[0] = 6 if bi == 7 else 7
        pb = bankbf(bi).rearrange("p (k t) -> p k t", k=8)
        for k in range(8):
            TR(pb[:, k, :], hn[:, k * 128:(k + 1) * 128], [rhn], [rbank[bi]])
        TT(hT[:, :, j * 128:(j + 1) * 128], pb,
           gcols[:, gi, :].unsqueeze(2).to_broadcast([128, 8, 128]), ALU.mult,
           [rbank[bi], r_gcols], [rhT[j]])

    def postnorm(tt, pi, gbi, fac, eps_mult=1.0):
        sl = tt % 16
        sc = stats[:, sl * 4:sl * 4 + 4]
        rs = r_stats[sl]
        pr = pairs[pi][:, :]
        rpp = [rbank[2 * pi], rbank[2 * pi + 1]]
        ACT(PT, pr, AF.Square, rpp, [r_pt, rs], accum_out=sc[:, 0:1])
        f2 = 1.0 / (fac * fac)
        TS(sc[:, 1:2], sc[:, 0:1], f2 / D, EPS * eps_mult * f2, ALU.mult, ALU.add, [rs], [rs])
        TT(sc[:, 2:3], sc[:, 1:2], neghalf, ALU.pow, [rs, r_neghalf], [rs], eng="pool")
        STT(PT, pr, sc[:, 2:3], GB[gbi], ALU.mult, ALU.mult, rpp + [rs, rGB[gbi]], [r_pt])
        TT(X[:, tt, :], X[:, tt, :], PT, ALU.add, [rX[tt], r_pt], [rX[tt]])

    def load_gain(gi, gbi):
        DMA("sp", GB[gbi], gains_d[gi:gi + 1, :].partition_broadcast(128), ("gb", gbi), [], [rGB[gbi]])

    F_ACT = B0 + 16384
    F_WD = F_ACT + 45056
    F_WGU = F_WD + 45056
    F_SIL = F_WGU + 12288
    assert F_SIL + 4096 == OFF_HN
    actT = view(F_ACT, 45056, BF16, "p (c t) -> p c t", c=NFC)
    WD = view(F_WD, 45056, BF16, "p (c n) -> p c n", c=NFC)
    WGU = [view(F_WGU + s * 4096, 4096, BF16, "p (g k n) -> p g k n", g=2, k=8) for s in range(3)]
    SIL = [view(F_SIL + i * 2048, 2048, F32) for i in range(2)]

    class FFNUnit:
        def __init__(self, l, which, b):
            self.l, self.which, self.b = l, which, b
            self.gi_pre = l * 6 + (0 if which == 0 else 4)
            self.gi_post = self.gi_pre + 1
            self.wg = wg_d[which][l]
            self.wu = wu_d[which][l]
            self.wd = wd_d[which][l]
            self.early_done = False

        def res(self):
            self.r_act = [RG(F_ACT + c * 2048 + n * 1024, 1024, "act%d_%d" % (c, n))[0]
                          for c in range(NFC) for n in range(2)]
            self.r_wd = [RG(F_WD + i * 4096, 4096, "wd%d" % i)[0] for i in range(NFC // 2)]
            self.r_wgu = [RG(F_WGU + s * 4096, 4096, "wg%d" % s, "wu%d" % s) for s in range(3)]
            self.r_sil = [RG(F_SIL + i * 2048, 2048, "sil%d" % i)[0] for i in range(2)]

        def load_gu(self, c):
            s = c % 3
            DMA("pool", WGU[s][:, 0, :, :], self.wg[:, c * 128:(c + 1) * 128].rearrange("(k p) n -> p k n", p=128),
                ("wg", s), [], [self.r_wgu[s][0]])
            DMA("pool", WGU[s][:, 1, :, :], self.wu[:, c * 128:(c + 1) * 128].rearrange("(k p) n -> p k n", p=128),
                ("wu", s), [], [self.r_wgu[s][1]])

        def early(self):
            self.r_wgu = [RG(F_WGU + s * 4096, 4096, "wg%d" % s, "wu%d" % s) for s in range(3)]
            for c in range(3):
                self.load_gu(c)
            self.early_done = True

        def stageA(self):
            if self.b == 0:
                load_gain(self.gi_post, 0)
            if not self.early_done:
                self.early()
            self.res()
            r_act, r_wd, r_wgu, r_sil = self.r_act, self.r_wd, self.r_wgu, self.r_sil
            nwd = 0
            for c in range(NFC):
                s = c % 3
                for n in range(2):
                    st_ = (2 * c + n) % 3
                    bg, bu = 2 * st_, 2 * st_ + 1
                    for k in range(8):
                        MM(bank(bg), WGU[s][:, 0, k, :], hT[:, k, n * 512:(n + 1) * 512], k == 0, k == 7,
                           [r_wgu[s][0]] + rhT[4 * n:4 * n + 4], [rbank[bg]])
                    for k in range(8):
                        MM(bank(bu), WGU[s][:, 1, k, :], hT[:, k, n * 512:(n + 1) * 512], k == 0, k == 7,
                           [r_wgu[s][1]] + rhT[4 * n:4 * n + 4], [rbank[bu]])
                    ACT(SIL[n], bank(bg), AF.Silu, [rbank[bg]], [r_sil[n]])
                    TT(actT[:, c, n * 512:(n + 1) * 512], SIL[n], bank(bu), ALU.mult,
                       [r_sil[n], rbank[bu]], [r_act[c * 2 + n]])
                if nwd < NFC // 2:
                    i = nwd
                    DMA("pool", WD[:, 2 * i:2 * i + 2, :],
                        self.wd[2 * i * 128:(2 * i + 2) * 128, :].rearrange("(c p) n -> p c n", p=128),
                        ("wd", i), [], [r_wd[i]])
                    nwd += 1
                if c + 3 < NFC:
                    self.load_gu(c + 3)

        def stageB(self, j):
            tt = self.b * BT + j
            pi = j % 3
            n = j // 4
            for d2 in range(2):
                for c in range(NFC):
                    MM(pairs[pi][:, d2 * 512:(d2 + 1) * 512], actT[:, c, j * 128:(j + 1) * 128],
                       WD[:, c, d2 * 512:(d2 + 1) * 512], c == 0, c == NFC - 1,
                       [self.r_act[c * 2 + n], self.r_wd[c // 2]], [rbank[2 * pi + d2]])
            postnorm(tt, pi, 0, 0.5)

    M_WA = B0 + 16384
    M_WATT = M_WA + 32768
    M_OGT = M_WATT + 12288
    M_AOT = M_OGT + 8192
    SCR = M_AOT + 8192
    assert SCR + 45056 == OFF_HN
    Whg = view(M_WA, 32768, BF16, "p (h k w n) -> p h k w n", h=4, k=8, w=4)
    Wout = view(M_WA, 16384, BF16, "p (k n) -> p k n", k=8)
    M3S = []
    for s in range(2):
        o = M_WA + 16384 + s * 6144
        M3S.append(dict(
            off=o,
            ga=view(o, 2048, BF16, "p (k n) -> p k n", k=8),
            gb=view(o + 2048, 2048, BF16, "p (k n) -> p k n", k=8),
            bh=view(o + 4096, 1024, BF16, "p (k n) -> p k n", k=4),
            ba=view(o + 5120, 1024, BF16, "p (k n) -> p k n", k=4),
        ))
    Watt = view(M_WATT, 12288, BF16, "p (k n) -> p k n", k=8)
    ogT = view(M_OGT, 8192, BF16, "p (h t) -> p h t", h=4)
    aoT = view(M_AOT, 8192, BF16, "p (h t) -> p h t", h=4)

    def sigmoid_chain3(specs):
        for (T_, src, rsrc, rT_, neg) in specs:
            ACT(T_, src, AF.Exp, rsrc, [rT_], scale=(1.0 if neg else -1.0))
        for (T_, src, rsrc, rT_, neg) in specs:
            ACT(T_, T_, AF.Ln, [rT_], [rT_], bias=1.0)
        for (T_, src, rsrc, rT_, neg) in specs:
            ACT(T_, T_, AF.Exp, [rT_], [rT_], scale=-1.0)

    def mixer_layer_setup(l):
        if l == 0:
            MEMSET("dve", omlc, 1.0, [r_lb])
            MEMSET("dve", nomlc, -1.0, [r_lb])
        else:
            TT(lbtmp, lbp[:, 1, :], lbp[:, 0, :], ALU.subtract, [r_lb], [r_lb])
            ACT(lbtmp, lbtmp, AF.Exp, [r_lb], [r_lb])
            TS(lbtmp, lbtmp, 1.0, None, ALU.add, None, [r_lb], [r_lb])
            P.add("dve", lambda e: e.reciprocal(out=omlc, in_=lbtmp), [r_lb], [r_lb])
            TS(nomlc, omlc, -1.0, None, ALU.mult, None, [r_lb], [r_lb])
        TS(homl, omlc, 0.5, None, ALU.mult, None, [r_lb], [r_lb])
        TS(nhoml, omlc, -0.5, None, ALU.mult, None, [r_lb], [r_lb])
        TS(bif, omlc, -0.5, 1.0, ALU.mult, ALU.add, [r_lb], [r_lb])
        TS(hg05, hgain[:, l:l + 1], 0.5, None, ALU.mult, None, [r_lb], [r_lb])
        ACT(lnhoml, homl, AF.Ln, [r_lb], [r_lb])
        DMA("sp", expsink, sinks_d[l:l + 1, :].partition_broadcast(128), "c_sink", [], [r_sink])
        ACT(expsink, expsink, AF.Exp, [r_sink], [r_sink])
        MEMSET("dve", state.rearrange("p h e -> p (h e)"), 0.0, r_state)
        load_gain(l * 6 + 3, 1)

    class MixUnit:
        def __init__(self, l, b):
            self.l, self.b = l, b
            self.gi_pre = l * 6 + 2
            self.early_done = True

        def early(self):
            pass

        def m1(self):
            l, b = self.l, self.b
            win = win_d[l]
            r_whg = [RG(M_WA + h * 8192, 8192, "whg%d" % h)[0] for h in range(4)]
            for h in range(4):
                DMA("pool", Whg[:, h, :, :, :].rearrange("p k w n -> p k (w n)"),
                    win[:, h * 512:(h + 1) * 512].rearrange("(k p) n -> p k n", p=128), ("whg", h), [], [r_whg[h]])
            self.r_watt = RG(M_WATT, 12288, "watt")[0]
            DMA("pool", Watt, win[:, 2048:2816].rearrange("(k p) n -> p k n", p=128), "watt", [], [self.r_watt])
            self.r_ogT = RG(M_OGT, 8192, *["ogT%d" % j for j in range(BT)])
            r_ogT = self.r_ogT

            Tsets = [[view(SCR + i * 2048, 2048, F32) for i in range(6)]] * 2
            rTsets = [[RG(SCR + i * 2048, 2048, "T%d" % i)[0] for i in range(6)]] * 2
            Dm = view(SCR + 12288, 4096, F32, "p (e c) -> p e c", c=8)
            Zb = view(SCR + 16384, 4096, F32, "p (e c) -> p e c", c=8)
            Zs = view(SCR + 20480, 4096, F32, "p (e c) -> p e c", c=8)
            r_Dm = RG(SCR + 12288, 4096, "Dm")[0]
            r_Zb = RG(SCR + 16384, 4096, "Zb")[0]
            r_Zs = RG(SCR + 20480, 4096, "Zs")[0]
            MEMSET("dve", Dm[:, :, 0:1], 0.0, [r_Dm])
            UB = SCR + 24576

            def ubuf(set_, i):
                return UB + set_ * 5120 + i * 1024
            qtT = [view(ubuf(s_, 0), 1024, BF16) for s_ in range(2)]
            ktT = [view(ubuf(s_, 1), 1024, BF16) for s_ in range(2)]
            kt = [view(ubuf(s_, 2), 1024, BF16, "p (j d) -> p j d", j=4) for s_ in range(2)]
            sgT = [view(ubuf(s_, 3), 1024, BF16) for s_ in range(2)]
            vtm = [view(ubuf(s_, 4), 1024, BF16, "p (j e) -> p j e", j=4) for s_ in range(2)]
            r_u = [[RG(ubuf(s_, i), 1024, "u%d_%d" % (s_, i))[0] for i in range(5)] for s_ in range(2)]
            o2 = SCR + 34816
            scm = [view(o2 + i * 1024, 1024, BF16, "p (j t) -> p j t", j=4) for i in range(2)]
            r_scm = [RG(o2 + i * 1024, 1024, "scm%d" % i)[0] for i in range(2)]
            sqb = view(o2 + 2048, 1024, BF16)
            r_sqb = RG(o2 + 2048, 1024, "sqb")[0]
            Rn = view(o2 + 3072, 2048, F32)
            r_Rn = RG(o2 + 3072, 2048, "Rn")[0]
            otmp = view(o2 + 5120, 2048, F32)
            r_otmp = RG(o2 + 5120, 2048, "otmp")[0]
            decay = [view(o2 + 7168 + i * 32, 32, F32) for i in range(2)]
            r_decay = [RG(o2 + 7168 + i * 32, 32, "decay%d" % i)[0] for i in range(2)]
            SbfA = view(o2 + 7232, 2048, BF16, "p (c e) -> p c e", c=8)
            r_SbfA = RG(o2 + 7232, 2048, "SbfA")[0]
            assert o2 + 7232 + 2048 <= SCR + 45056

            def proj_v(u, h, g):
                tok0 = g * 512
                bv = bank(3).rearrange("p (j e) -> p j e", j=4)
                for jj in range(4):
                    for k in range(8):
                        MM(bv[:, jj, :], hT[:, k, tok0 + jj * 128: tok0 + (jj + 1) * 128], Whg[:, h, k, 2, :],
                           k == 0, k == 7, [rhT[4 * g + jj], r_whg[h]], [rbank[3]])
                    yield

            def proj_qfg(u, h, g):
                tok0 = g * 512
                rh = rhT[4 * g:4 * g + 4]
                n = 0
                for (bi, wi) in ((0, 0), (1, 1), (2, 3)):
                    for k in range(8):
                        MM(bank(bi), Whg[:, h, k, wi, :], hT[:, k, tok0:tok0 + 512], k == 0, k == 7,
                           rh + [r_whg[h]], [rbank[bi]])
                        n += 1
                        if n % 3 == 0:
                            yield

            def elem(u, h, g):
                s_ = u % 2
                T = Tsets[s_]
                rT = rTsets[s_]
                rq, rk, rkt, rsg, rv = r_u[s_]
                bv = bank(3).rearrange("p (j e) -> p j e", j=4)
                CP("act", vtm[s_], bv, [rbank[3]], [rv])
                ACT(T[0], bank(0), AF.Tanh, [rbank[0]], [rT[0]], scale=0.5)
                ACT(T[2], bank(1), AF.Tanh, [rbank[1]], [rT[2]], scale=-0.5)
                ACT(T[3], bank(2), AF.Tanh, [rbank[2]], [rT[3]], scale=0.5)
                ACT(T[4], T[2], AF.Ln, [rT[2], r_lb], [rT[4]], scale=nhoml[:, h:h + 1], bias=bif[:, h:h + 1])
                yield
                T1b = view(SCR + 2048, 1024, BF16)
                STT(T1b, T[0], 1.0, bank(0), ALU.add, ALU.mult, [rT[0], rbank[0]], [rT[1]])
                STT(sgT[s_], T[3], 1.0, bank(2), ALU.add, ALU.mult, [rT[3], rbank[2]], [rsg])
                yield "banks_free"
                P.add("dve", lambda e, o=T[5], d1=T[4]: e.tensor_tensor_scan(
                    out=o, data0=notstart, data1=d1, initial=0.0, op0=ALU.mult, op1=ALU.add),
                    [rT[4], r_ns], [rT[5]])
                yield
                ACT(T[4], T[5], AF.Exp, [rT[5]], [rT[4]])
                ACT(T[0], T[5], AF.Exp, [rT[5], r_lb], [rT[0]], scale=-1.0, bias=lnhoml[:, h:h + 1])
                ACT(decay[s_], T[5].rearrange("p (c t) -> p c t", t=64)[:, :, 63], AF.Exp, [rT[5]], [r_decay[s_]])
                yield
                STT(qtT[s_], T1b, 0.5, T[4], ALU.mult, ALU.mult, [rT[1], rT[4]], [rq])
                yield
                STT(ktT[s_], T[2], 1.0, T[0], ALU.add, ALU.mult, [rT[2], rT[0]], [rk])
                yield
                pb = bankbf(3).rearrange("p (j d) -> p j d", j=8)
                for jj in range(4):
                    TR(pb[:, jj, :], ktT[s_][:, jj * 128:(jj + 1) * 128], [rk], [rbank[3]])
                CP("act", kt[s_], pb[:, 0:4, :], [rbank[3]], [rkt])
                yield

            def rec(u, h, g):
                s_ = u % 2
                rq, rk, rkt, rsg, rv = r_u[s_]
                bs = bank(4).rearrange("p (j t) -> p j t", j=4)
                for jj in range(4):
                    tsl = slice(jj * 128, (jj + 1) * 128)
                    MM(bs[:, jj, :], ktT[s_][:, tsl], qtT[s_][:, tsl], True, True, [rk, rq], [rbank[4]])
                sc_ = scm[s_]
                TT(sc_, bs, maskc.unsqueeze(1).to_broadcast([128, 4, 128]), ALU.mult,
                   [rbank[4], r_maskc], [r_scm[s_]])
                bo = bank(5)
                ci = 0
                for ch in range(2):
                    ub = bank(6 + ch).rearrange("p (c e) -> p c e", c=4)
                    ps = slice(ch * 64, (ch + 1) * 64)
                    for jj in range(4):
                        MM(ub[:, jj, :], kt[s_][ps, jj, :], vtm[s_][ps, jj, :], True, True, [rkt, rv], [rbank[6 + ch]])
                yield
                if RECLVL < 2:
                    return
                dec = decay[s_]
                CP("act", SbfA[:, 0, :], state[:, h, :], [r_state[h]], [r_SbfA])
                CP("dve", Dm[:, :, 1:8], dec[:, 1:8].unsqueeze(1).to_broadcast([128, 128, 7]), [r_decay[s_]], [r_Dm])
                ub0 = bank(6).rearrange("p (c e) -> p c e", c=4)
                TT(ub0[:, 0, :], ub0[:, 0, :], state[:, h, :], ALU.add, [rbank[6], r_state[h]], [rbank[6]])
                Zb_v = Zb.rearrange("p e (jj ch) -> p ch jj e", ch=2)
                for ch in range(2):
                    ub = bank(6 + ch).rearrange("p (c e) -> p c e", c=4)
                    TT(Zb_v[:, ch], ub, dec[:, ch::2].unsqueeze(2).to_broadcast([128, 4, 128]), ALU.mult,
                       [rbank[6 + ch], r_decay[s_]], [r_Zb])
                yield
                P.add("dve", lambda e, o=Zs.rearrange("p e c -> p (e c)"), d0=Dm.rearrange("p e c -> p (e c)"),
                      d1=Zb.rearrange("p e c -> p (e c)"): e.tensor_tensor_scan(
                    out=o, data0=d0, data1=d1, initial=0.0, op0=ALU.mult, op1=ALU.add),
                    [r_Dm, r_Zb], [r_Zs])
                yield
                CP("act", SbfA[:, 1:8, :], Zs.rearrange("p e c -> p c e")[:, 0:7, :], [r_Zs], [r_SbfA])
                CP("dve", state[:, h, :], Zs[:, :, 7], [r_Zs], [r_state[h]])
                yield
                for c in range(8):
                    jj, ch = c // 2, c % 2
                    ps = slice(ch * 64, (ch + 1) * 64)
                    cols = slice(c * 64, (c + 1) * 64)
                    MM(bo[:, cols], vtm[s_][ps, jj, :], sc_[ps, jj, ps], True, False, [rv, r_scm[s_]], [rbank[5]])
                    MM(bo[:, cols], SbfA[:, c, :], qtT[s_][:, cols], False, True, [r_SbfA, rq], [rbank[5]])
                    if c % 2 == 1:
                        yield
                if RECLVL < 3:
                    return
                ACT(sqb, bo, AF.Square, [rbank[5]], [r_sqb])
                MM(bank(4), ones_bf, sqb, True, True, [r_ones, r_sqb], [rbank[4]])
                ACT(Rn, bank(4), AF.Ln, [rbank[4]], [r_Rn], scale=1.0 / 128, bias=EPS)
                ACT(Rn, Rn, AF.Exp, [r_Rn], [r_Rn], scale=-0.5)
                STT(otmp, bo, hg05[:, 0:1], Rn, ALU.mult, ALU.mult, [rbank[5], r_Rn, r_lb], [r_otmp])
                TT(ogT[:, h, g * 512:(g + 1) * 512], otmp, sgT[s_], ALU.mult, [r_otmp, rsg], r_ogT[4 * g:4 * g + 4])
                yield

            units = [(h, g) for h in range(4) for g in range(2)]
            NU = len(units)

            def drain(gen):
                for _ in gen:
                    pass

            def step(gen):
                try:
                    return next(gen), False
                except StopIteration:
                    return None, True

            drain(proj_v(0, *units[0]))
            drain(proj_qfg(0, *units[0]))
            drain(elem(0, *units[0]))
            if NU > 1:
                drain(proj_v(1, *units[1]))
                drain(proj_qfg(1, *units[1]))
            for i in range(NU):
                g_rec = rec(i, *units[i]) if MIXLVL >= 2 else iter(())
                g_el = elem(i + 1, *units[i + 1]) if i + 1 < NU else iter(())
                g_pq = proj_qfg(i + 2, *units[i + 2]) if i + 2 < NU else iter(())
                d_rec = d_el = d_pq = False
                pq_ok = False
                while not (d_rec and d_el and d_pq):
                    if not d_rec:
                        _, d_rec = step(g_rec)
                    if not d_el:
                        tag, d_el = step(g_el)
                        if tag == "banks_free" or d_el:
                            pq_ok = True
                    if i + 1 >= NU:
                        pq_ok = True
                    if pq_ok and not d_pq:
                        _, d_pq = step(g_pq)
                        if not d_pq:
                            _, d_pq = step(g_pq)
                if i + 2 < NU:
                    drain(proj_v(i + 2, *units[i + 2]))

        def m2(self):
            l, b = self.l, self.b
            r_watt = self.r_watt
            self.r_aoT = RG(M_AOT, 8192, *["aoT%d" % j for j in range(BT)])
            r_aoT = self.r_aoT
            qT = view(SCR + 0, 8192, BF16, "p (c t) -> p c t", c=4)
            kT = view(SCR + 8192, 2304, BF16)
            Vaug = view(SCR + 10752, 2560, BF16)[:, 0:1170].rearrange("p (j k d) -> p j k d", j=9, k=2)
            abst = view(SCR + 13312, 8192, F32)
            expb = view(SCR + 21504, 4096, BF16, "p (b h q) -> p b h q", b=2, h=8)
            etmp = [view(SCR + 25600 + i * 2048, 1024, BF16) for i in range(2)]
            pT = [[[view(SCR + 29696 + ((s_ * 2 + kv) * 2 + kb) * 1024, 1024, BF16) for kb in range(2)]
                   for kv in range(2)] for s_ in range(2)]
            ao = [view(SCR + 37888 + i * 1024, 1024, BF16) for i in range(2)]
            den = view(SCR + 39936, 128, F32)
            r_qT = RG(SCR + 0, 8192, "qT0", "qT1")
            r_kT = RG(SCR + 8192, 2304, "kTp", "kT0", "kT1")
            r_V = RG(SCR + 10752, 2560, *["V%d" % j_ for j_ in range(9)])
            r_abst = RG(SCR + 13312, 8192, "abst")[0]
            r_expb = RG(SCR + 21504, 4096, "expb")[0]
            r_etmp = [RG(SCR + 25600 + i * 2048, 2048, "etmp%d" % i)[0] for i in range(2)]
            r_pT = [[[RG(SCR + 29696 + ((s_ * 2 + kv) * 2 + kb) * 1024, 1024, "pT%d%d%d" % (s_, kv, kb))[0]
                      for kb in range(2)] for kv in range(2)] for s_ in range(2)]
            r_ao = [RG(SCR + 37888 + i * 1024, 1024, "ao%d" % i)[0] for i in range(2)]
            r_den = RG(SCR + 39936, 128, "den")[0]

            DMA("sp", abst, abias_d, "abias", [], [r_abst])
            ACT(expb.rearrange("p b h q -> p (b h q)"), abst, AF.Exp, [r_abst], [r_expb])
            MEMSET("dve", Vaug[:, :, :, 64], 1.0, r_V)
            if b > 0:
                CP("dve", kT[:, 0:128], kTprev, [r_kTprev], [r_kT[0]])
                CP("dve", Vaug[:, 0, :, :], Vprev, [r_Vprev], [r_V[0]])
            pbi = 6
            for g in range(2):
                rh = rhT[4 * g:4 * g + 4]
                for cc in range(4):
                    for k in range(8):
                        MM(bank(pbi), Watt[:, k, cc * 128:(cc + 1) * 128], hT[:, k, g * 512:(g + 1) * 512],
                           k == 0, k == 7, rh + [r_watt], [rbank[pbi]])
                    CP("act" if cc % 2 == 0 else "dve", qT[:, cc, g * 512:(g + 1) * 512], bank(pbi),
                       [rbank[pbi]], [r_qT[g]])
                    pbi = 13 - pbi
                for k in range(8):
                    MM(bank(pbi), Watt[:, k, 512:640], hT[:, k, g * 512:(g + 1) * 512],
                       k == 0, k == 7, rh + [r_watt], [rbank[pbi]])
                CP("act", kT[:, 128 + g * 512: 128 + (g + 1) * 512], bank(pbi), [rbank[pbi]], [r_kT[1 + g]])
                pbi = 13 - pbi
                bv = bank(pbi).rearrange("p (j n) -> p j n", j=4)
                for jj in range(4):
                    for k in range(8):
                        MM(bv[:, jj, :], hT[:, k, g * 512 + jj * 128: g * 512 + (jj + 1) * 128], Watt[:, k, 640:768],
                           k == 0, k == 7, [rhT[4 * g + jj], r_watt], [rbank[pbi]])
                P.add("dve", lambda e, o=Vaug[:, 1 + 4 * g:5 + 4 * g, :, 0:64],
                      i_=bv.rearrange("p j (k d) -> p j k d", k=2): e.tensor_copy(out=o, in_=i_),
                      [rbank[pbi]], r_V[1 + 4 * g:5 + 4 * g])
                pbi = 13 - pbi
            if b == 0:
                CP("dve", kTprev, kT[:, 1024:1152], [r_kT[2]], [r_kTprev])
                CP("dve", Vprev, Vaug[:, 8, :, :], [r_V[8]], [r_Vprev])
            def att_stage1(j):
                tt = b * BT + j
                s_ = j % 2
                kbs = [1] if tt == 0 else [0, 1]
                for kv in range(2):
                    prt = slice(kv * 64, (kv + 1) * 64)
                    for kb in kbs:
                        bi = (kv * 2 + kb)
                        kc0 = (j + kb) * 128
                        MM(bank(bi), kT[prt, kc0:kc0 + 128], qT[prt, :, j * 128:(j + 1) * 128], True, True,
                           r_kT + [r_qT[j // 4]], [rbank[bi]])
                        et = etmp[(kv * 2 + kb) % 2]
                        ret = r_etmp[(kv * 2 + kb) % 2]
                        ACT(et, bank(bi), AF.Exp, [rbank[bi]], [ret], scale=0.125)
                        TT(pT[s_][kv][kb].rearrange("p (h q) -> p h q", h=4), et.rearrange("p (h q) -> p h q", h=4),
                           expb[:, kb, kv * 4:(kv + 1) * 4, :], ALU.mult, [ret, r_expb], [r_pT[s_][kv][kb]])

            def att_stage2(j):
                tt = b * BT + j
                s_ = j % 2
                kbs = [1] if tt == 0 else [0, 1]
                for kv in range(2):
                    bpv = bank(4 + kv)[:, 0:260].rearrange("p (h d) -> p h d", h=4)
                    for g4 in range(4):
                        for n_, kb in enumerate(kbs):
                            MM(bpv[:, g4, :], pT[s_][kv][kb][:, g4 * 128:(g4 + 1) * 128], Vaug[:, j + kb, kv, :],
                               n_ == 0, n_ == len(kbs) - 1, [r_pT[s_][kv][kb], r_V[j + kb]], [rbank[4 + kv]])
                    dn = den[:, kv * 8:kv * 8 + 4]
                    rd = den[:, kv * 8 + 4:kv * 8 + 8]
                    TT(dn, bpv[:, :, 64], expsink[:, kv * 4:(kv + 1) * 4], ALU.add, [rbank[4 + kv], r_sink], [r_den])
                    P.add("dve", lambda e, o=rd, i_=dn: e.reciprocal(out=o, in_=i_), [r_den], [r_den])
                    TT(ao[s_][:, kv * 256:(kv + 1) * 256].rearrange("p (h d) -> p h d", h=4), bpv[:, :, 0:64],
                       rd.unsqueeze(2).to_broadcast([128, 4, 64]), ALU.mult, [rbank[4 + kv], r_den], [r_ao[s_]])
                tb = 6 + (j % 2)
                pb = bankbf(tb).rearrange("p (c t) -> p c t", c=8)
                for cc in range(4):
                    TR(pb[:, cc, :], ao[s_][:, cc * 128:(cc + 1) * 128], [r_ao[s_]], [rbank[tb]])
                CP("act", aoT[:, :, j * 128:(j + 1) * 128], pb[:, 0:4, :], [rbank[tb]], [r_aoT[j]])


            att_stage1(0)
            for j in range(BT):
                if j + 1 < BT:
                    att_stage1(j + 1)
                att_stage2(j)

        def m3(self):
            l, b = self.l, self.b
            win = win_d[l]
            r_ogT, r_aoT = self.r_ogT, self.r_aoT
            self.mixT = view(SCR + 0, 16384, BF16, "p (c t) -> p c t", c=8)
            mixT = self.mixT
            self.r_mixT = [RG(SCR + c_ * 2048 + g_ * 1024, 1024, "mixT%d_%d" % (c_, g_))[0]
                           for c_ in range(8) for g_ in range(2)]
            r_mixT = self.r_mixT
            TM = [[view(SCR + 16384 + (s2 * 4 + i) * 2048, 2048, F32) for i in range(4)] for s2 in range(2)]
            r_TM = [[RG(SCR + 16384 + (s2 * 4 + i) * 2048, 2048, "TM%d_%d" % (s2, i))[0] for i in range(4)]
                    for s2 in range(2)]
            self.r_wout = [RG(M_WA + i * 8192, 8192, "wout%d" % i)[0] for i in range(2)]
            for i in range(2):
                DMA("pool", Wout[:, 4 * i:4 * i + 4, :],
                    wout_d[l][i * 512:(i + 1) * 512, :].rearrange("(k p) n -> p k n", p=128), ("wout", i), [], [self.r_wout[i]])
            r_m3s = [dict(zip(("ga", "gb", "bh", "ba"),
                              [RG(M3S[s]["off"], 2048, "ga%d" % s)[0], RG(M3S[s]["off"] + 2048, 2048, "gb%d" % s)[0],
                               RG(M3S[s]["off"] + 4096, 1024, "bh%d" % s)[0], RG(M3S[s]["off"] + 5120, 1024, "ba%d" % s)[0]]))
                     for s in range(2)]

            def load_m3(c):
                s = c % 2
                DMA("pool", M3S[s]["ga"], win[:, 2816 + c * 128: 2816 + (c + 1) * 128].rearrange("(k p) n -> p k n", p=128),
                    ("ga", s), [], [r_m3s[s]["ga"]])
                DMA("pool", M3S[s]["gb"], win[:, 3840 + c * 128: 3840 + (c + 1) * 128].rearrange("(k p) n -> p k n", p=128),
                    ("gb", s), [], [r_m3s[s]["gb"]])
                DMA("pool", M3S[s]["bh"], wbh_d[l][:, c * 128:(c + 1) * 128].rearrange("(k p) n -> p k n", p=128),
                    ("bh", s), [], [r_m3s[s]["bh"]])
                DMA("pool", M3S[s]["ba"], wba_d[l][:, c * 128:(c + 1) * 128].rearrange("(k p) n -> p k n", p=128),
                    ("ba", s), [], [r_m3s[s]["ba"]])

            load_m3(0)
            load_m3(1)
            it = 0
            for c in range(8):
                s = c % 2
                for g in range(2):
                    s2 = it % 2
                    it += 1
                    b4 = 4 * s2
                    rh = rhT[4 * g:4 * g + 4]
                    tsl = slice(g * 512, (g + 1) * 512)
                    for k in range(8):
                        MM(bank(b4), M3S[s]["ga"][:, k, :], hT[:, k, tsl], k == 0, k == 7, rh + [r_m3s[s]["ga"]], [rbank[b4]])
                    for k in range(8):
                        MM(bank(b4 + 1), M3S[s]["gb"][:, k, :], hT[:, k, tsl], k == 0, k == 7, rh + [r_m3s[s]["gb"]], [rbank[b4 + 1]])
                    for k in range(4):
                        MM(bank(b4 + 2), M3S[s]["bh"][:, k, :], ogT[:, k, tsl], k == 0, k == 3,
                           r_ogT[4 * g:4 * g + 4] + [r_m3s[s]["bh"]], [rbank[b4 + 2]])
                    for k in range(4):
                        MM(bank(b4 + 3), M3S[s]["ba"][:, k, :], aoT[:, k, tsl], k == 0, k == 3,
                           r_aoT[4 * g:4 * g + 4] + [r_m3s[s]["ba"]], [rbank[b4 + 3]])
                    Ta, Tb, U1, U2 = TM[s2]
                    rTa, rTb, rU1, rU2 = r_TM[s2]
                    ACT(Ta, bank(b4), AF.Tanh, [rbank[b4]], [rTa], scale=0.5)
                    ACT(Tb, bank(b4 + 1), AF.Tanh, [rbank[b4 + 1]], [rTb], scale=0.5)
                    STT(U1, Ta, 1.0, bank(b4 + 2), ALU.add, ALU.mult, [rTa, rbank[b4 + 2]], [rU1])
                    STT(U2, Tb, 1.0, bank(b4 + 3), ALU.add, ALU.mult, [rTb, rbank[b4 + 3]], [rU2])
                    TT(mixT[:, c, tsl], U1, U2, ALU.add, [rU1, rU2], [r_mixT[c * 2 + g]])
                if c + 2 < 8:
                    load_m3(c + 2)

        def stageA(self):
            if self.b == 0:
                mixer_layer_setup(self.l)
            self.m1()
            if MIXLVL >= 3:
                self.m2()
            if MIXLVL >= 4:
                self.m3()

        def stageB(self, j):
            if MIXLVL < 4:
                return
            tt = self.b * BT + j
            pi = j % 3
            g = j // 4
            for d2 in range(2):
                for c in range(8):
                    MM(pairs[pi][:, d2 * 512:(d2 + 1) * 512], self.mixT[:, c, j * 128:(j + 1) * 128],
                       Wout[:, c, d2 * 512:(d2 + 1) * 512], c == 0, c == 7,
                       [self.r_mixT[c * 2 + g], self.r_wout[c // 4]], [rbank[2 * pi + d2]])
            postnorm(tt, pi, 1, 1.0, eps_mult=4.0)

    units = []
    for l in range(DEPTH):
        names = ["L%d_0" % l, "L%d_1" % l, "L%d_2" % l]
        units += [FFNUnit(l, 0, 0), FFNUnit(l, 0, 1)]
        if stop_after == names[0]:
            break
        units += [MixUnit(l, 0), MixUnit(l, 1)]
        if stop_after == names[1]:
            break
        units += [FFNUnit(l, 1, 0), FFNUnit(l, 1, 1)]
        if stop_after == names[2]:
            break

    u0 = units[0]
    prenorm_block(u0.b, u0.gi_pre)
    for ui, u in enumerate(units):
        nxt = units[ui + 1] if ui + 1 < len(units) else None
        u.stageA()
        if nxt is not None and HOIST_PRE:
            prenorm_elem(nxt.b, 0)
        for j in range(BT):
            if nxt is not None and HOIST_PRE:
                if j + 1 < BT:
                    prenorm_elem(nxt.b, j + 1)
                prenorm_tr(nxt.b, j, nxt.gi_pre)
            u.stageB(j)
            if HOIST_STAT and nxt is not None:
                prenorm_stat(nxt.b, j, HN[0], r_hn[0])
            if HOIST_EARLY and nxt is not None and j == 3 and isinstance(nxt, FFNUnit):
                nxt.early()

        if nxt is not None and not HOIST_PRE:
            prenorm_block(nxt.b, nxt.gi_pre, stats_done=HOIST_STAT)

    fin = []
    for t in range(NT):
        fin.append(DMA("sp", out_d[t * 128:(t + 1) * 128, :], X[:, t, :], ("x", t), [rX[t]], []))
    P.emit(nc, final_waits=fin)
    st.close()
    return nc


def _consts():
    ident = np.eye(128, dtype=np.float32)
    s = np.arange(128)[:, None]
    t = np.arange(128)[None, :]
    maskc = ((s // 64 == t // 64) & (s <= t)).astype(np.float32)
    notstart = np.ones((128, 512), np.float32)
    notstart[:, ::64] = 0.0
    slopes = np.exp2(-8.0 * np.arange(1, 9, dtype=np.float32) / 8.0)
    k = np.arange(128)[:, None, None]
    q = np.arange(128)[None, None, :]
    ab = np.empty((128, 2, 8, 128), np.float32)
    for kb in range(2):
        dist = (q - k + (128 if kb == 0 else 0)).astype(np.float32)
        valid = (k > q) if kb == 0 else (k <= q)
        ab[:, kb] = np.where(valid, -slopes[None, :, None] * dist, -30000.0)
    return ident, maskc, notstart, ab.reshape(128, -1)


_PROG_CACHE = {}


def kernel(x, norm_gains, w_ffn1_gate, w_ffn1_up, w_ffn1_down, w_in, hgrn_lower_bounds,
           hgrn_head_gain, attn_sinks, w_branch_hgrn, w_branch_attn, w_out,
           w_ffn2_gate, w_ffn2_up, w_ffn2_down, _stop_after=None):
    f = lambda a: np.ascontiguousarray(np.asarray(a, dtype=np.float32))
    x = f(x)
    ng = f(norm_gains)
    gains = ng.reshape(12, D)
    gcols = np.ascontiguousarray(ng.reshape(12, 8, 128).transpose(2, 0, 1))
    w_in = f(w_in).copy()
    perm = []
    for c in range(4):
        perm += list(range(2048 + c * 64, 2048 + (c + 1) * 64))
        perm += list(range(2048 + (4 + c) * 64, 2048 + (5 + c) * 64))
    w_in[:, :, 2048:2560] = w_in[:, :, perm]
    permh = [w_ * 512 + h_ * 128 + n_ for h_ in range(4) for w_ in range(4) for n_ in range(128)]
    w_in[:, :, 0:2048] = w_in[:, :, permh]
    lbp = np.ascontiguousarray(f(hgrn_lower_bounds).reshape(2, 4, 128).transpose(2, 0, 1))
    hgain = np.ascontiguousarray(f(hgrn_head_gain).T)
    ident, maskc, notstart, abias = _consts()
    shared = {
        "gcols": gcols, "gains": gains,
        "wg1": f(w_ffn1_gate), "wu1": f(w_ffn1_up), "wd1": f(w_ffn1_down),
        "wg2": f(w_ffn2_gate), "wu2": f(w_ffn2_up), "wd2": f(w_ffn2_down),
        "w_in": w_in, "w_bh": f(w_branch_hgrn), "w_ba": f(w_branch_attn), "w_out": f(w_out),
        "lbp": lbp, "hgain": hgain, "sinks": f(attn_sinks),
        "ident": ident, "maskc": maskc, "notstart": notstart, "abias": abias,
    }
    key = _stop_after
    if key not in _PROG_CACHE:
        _PROG_CACHE[key] = build_program(_stop_after)
    nc = _PROG_CACHE[key]
    in_maps = [dict(shared, x=x[i]) for i in range(8)]
    res = run_bass_kernel_spmd(nc, in_maps, core_ids=list(range(8)))
    return np.stack([np.asarray(r["out"], dtype=np.float32) for r in res.results], axis=0)
```
